# Optimizing a Trainium2 kernel written in Bass

```python
import math
import jax, jax.numpy as jnp
from jax import lax
import numpy as np

D_MODEL = 2048
BATCH = 16
SEQ = 2048
DEPTH = 2

N_A = DEPTH // 2
N_B = DEPTH - N_A
CONV_WIDTH = 31
HEAD_DIM = 64
N_HEADS = D_MODEL // HEAD_DIM
N_KV_HEADS = 8
GROUP = N_HEADS // N_KV_HEADS
WINDOW = 128
BLOCK = 128
D_FF = 4 * D_MODEL
NORM_EPS = 1e-6
LN_EPS = 1e-5

kernel_name = "yoco_conformer_swa_sink_hybrid"


def rms_norm(x, g):
    xf = x.astype(jnp.float32)
    y = xf * lax.rsqrt(jnp.mean(xf * xf, axis=-1, keepdims=True) + NORM_EPS)
    return (y * g.astype(jnp.float32)).astype(x.dtype)


def layer_norm(x, g, b):
    xf = x.astype(jnp.float32)
    mu = jnp.mean(xf, axis=-1, keepdims=True)
    var = jnp.mean(jnp.square(xf - mu), axis=-1, keepdims=True)
    y = (xf - mu) * lax.rsqrt(var + LN_EPS)
    return (y * g.astype(jnp.float32) + b.astype(jnp.float32)).astype(x.dtype)


def conformer_conv(h, w_in, b_in, w_dw, b_dw, ln_g, ln_b, w_out, b_out):
    u = h @ w_in + b_in
    a, gate = jnp.split(u, 2, axis=-1)
    u = a * jax.nn.sigmoid(gate)
    u = lax.conv_general_dilated(
        u, w_dw[:, None, :].astype(u.dtype), window_strides=(1,),
        padding=((CONV_WIDTH - 1, 0),),
        dimension_numbers=("NWC", "WIO", "NWC"),
        feature_group_count=D_MODEL) + b_dw
    u = jax.nn.silu(layer_norm(u, ln_g, ln_b))
    return u @ w_out + b_out


def swa_sink_attention(h, w_q, b_q, sinks, w_o, b_o, k, v):
    B, S, _ = h.shape
    nb = S // BLOCK
    q = (h @ w_q + b_q).reshape(B, nb, BLOCK, N_KV_HEADS, GROUP, HEAD_DIM)

    def band(t):
        tb = t.reshape(B, nb, BLOCK, N_KV_HEADS, HEAD_DIM)
        prev = jnp.concatenate([jnp.zeros_like(tb[:, :1]), tb[:, :-1]], axis=1)
        return jnp.concatenate([prev, tb], axis=2)

    kw = jnp.moveaxis(band(k), 1, 0)
    vw = jnp.moveaxis(band(v), 1, 0)
    qm = jnp.moveaxis(q, 1, 0)
    scale = 1.0 / math.sqrt(HEAD_DIM)
    sink_logit = sinks.astype(jnp.float32).reshape(N_KV_HEADS, GROUP)

    def one_block(args):
        n, qb, kb, vb = args
        s = jnp.einsum("bqkgd,bckd->bkgqc", qb, kb).astype(jnp.float32) * scale
        q_pos = n * BLOCK + jnp.arange(BLOCK)
        k_pos = (n - 1) * BLOCK + jnp.arange(2 * BLOCK)
        diff = q_pos[:, None] - k_pos[None, :]
        mask = (diff >= 0) & (diff < WINDOW) & (k_pos[None, :] >= 0)
        s = jnp.where(mask, s, -jnp.inf)
        sink = jnp.broadcast_to(sink_logit[None, :, :, None, None], s.shape[:-1] + (1,))
        p = jax.nn.softmax(jnp.concatenate([s, sink], axis=-1), axis=-1)[..., :-1]
        return jnp.einsum("bkgqc,bckd->bqkgd", p.astype(vb.dtype), vb)

    o = lax.map(one_block, (jnp.arange(nb), qm, kw, vw))
    o = jnp.moveaxis(o, 0, 1).reshape(B, S, N_HEADS * HEAD_DIM)
    return o @ w_o + b_o


def sqrelu_mlp(h, w_up, w_down):
    return jnp.square(jax.nn.relu(h @ w_up)) @ w_down


def setup_inputs(seed: int = 0) -> dict:
    key = jax.random.key(seed)
    ks = iter(jax.random.split(key, 32))
    f32 = jnp.float32

    def w(shape, fan_in):
        return jax.random.normal(next(ks), shape, f32) * (fan_in ** -0.5)

    def gain(shape):
        return 1.0 + 0.02 * jax.random.normal(next(ks), shape, f32)

    def bias(shape):
        return 0.02 * jax.random.normal(next(ks), shape, f32)

    D, HD_Q, HD_KV = D_MODEL, N_HEADS * HEAD_DIM, N_KV_HEADS * HEAD_DIM
    return {
        "x": jax.random.normal(next(ks), (BATCH, SEQ, D), f32),
        "a_norm": gain((N_A, D)),
        "a_w_in": w((N_A, D, 2 * D), D),
        "a_b_in": bias((N_A, 2 * D)),
        "a_w_dw": w((N_A, CONV_WIDTH, D), CONV_WIDTH),
        "a_b_dw": bias((N_A, D)),
        "a_ln_g": gain((N_A, D)),
        "a_ln_b": bias((N_A, D)),
        "a_w_out": w((N_A, D, D), D),
        "a_b_out": bias((N_A, D)),
        "kv_norm": gain((D,)),
        "w_k": w((D, HD_KV), D),
        "b_k": bias((HD_KV,)),
        "w_v": w((D, HD_KV), D),
        "b_v": bias((HD_KV,)),
        "b_norm": gain((N_B, D)),
        "b_w_q": w((N_B, D, HD_Q), D),
        "b_b_q": bias((N_B, HD_Q)),
        "b_sinks": 0.5 * jax.random.normal(next(ks), (N_B, N_HEADS), f32),
        "b_w_o": w((N_B, HD_Q, D), HD_Q),
        "b_b_o": bias((N_B, D)),
        "mlp_norm": gain((DEPTH, D)),
        "mlp_w_up": w((DEPTH, D, D_FF), D),
        "mlp_w_down": w((DEPTH, D_FF, D), D_FF),
        "final_norm": gain((D,)),
    }


def reference(x, a_norm, a_w_in, a_b_in, a_w_dw, a_b_dw, a_ln_g, a_ln_b, a_w_out, a_b_out,
              kv_norm, w_k, b_k, w_v, b_v,
              b_norm, b_w_q, b_b_q, b_sinks, b_w_o, b_b_o,
              mlp_norm, mlp_w_up, mlp_w_down, final_norm):
    B, S, _ = x.shape
    h = x
    k = v = None
    for i in range(DEPTH):
        if i < N_A:
            h = h + conformer_conv(rms_norm(h, a_norm[i]), a_w_in[i], a_b_in[i], a_w_dw[i],
                                   a_b_dw[i], a_ln_g[i], a_ln_b[i], a_w_out[i], a_b_out[i])
        else:
            j = i - N_A
            if j == 0:
                hk = rms_norm(h, kv_norm)
                k = (hk @ w_k + b_k).reshape(B, S, N_KV_HEADS, HEAD_DIM)
                v = (hk @ w_v + b_v).reshape(B, S, N_KV_HEADS, HEAD_DIM)
            h = h + swa_sink_attention(rms_norm(h, b_norm[j]), b_w_q[j], b_b_q[j], b_sinks[j],
                                       b_w_o[j], b_b_o[j], k, v)
        h = h + sqrelu_mlp(rms_norm(h, mlp_norm[i]), mlp_w_up[i], mlp_w_down[i])
    return rms_norm(h, final_norm)
```

```python
import numpy as np
import concourse.bass as bass
import concourse.mybir as mybir
from concourse.bass_utils import run_bass_kernel_spmd
F32 = mybir.dt.float32
BF16 = mybir.dt.bfloat16
AF = mybir.ActivationFunctionType
ALU = mybir.AluOpType
P = 128
D = 2048
KC = D // P
T = 512
NTB = T // P
SEQ = 2048
NSEQ = 2
TPS = SEQ // T
DFF = 8192
HC = DFF // P
CW = 31
HALO = CW - 1
NG = 8
UW = 256
NSLOT = 4
NORM_EPS = 1e-06
LN_EPS = 1e-05
_c = {}
_off = 0
for _name, _n in [('a_norm', 16), ('b_in_a', 16), ('b_in_g', 16), ('w_dw', CW * 16), ('b_dw', 16), ('ln_g', 16), ('ln_b', 16), ('b_out', 16), ('kv_norm', 16), ('b_k', 8), ('b_norm', 16), ('b_q', 16), ('b_o', 16), ('mlp_norm0', 16), ('mlp_norm1', 16), ('final_norm', 16), ('eps_n', 1), ('eps_ln', 1)]:
    _c[_name] = _off
    _off += _n
NCONST = _off

class Sched:
    ENG = ('pe', 'act', 'dve', 'pool', 'sp')

    def __init__(self):
        self.ops = {e: [] for e in self.ENG}
        self.last_w = {}
        self.readers = {}
        self.dma_cnt = {}

    def op(self, eng, name, kw, reads=(), writes=(), dma=None):
        fn = (name, kw)
        ops = self.ops[eng]
        idx = len(ops)
        deps = set()

        def add(dep, kind):
            if dep is None:
                return
            if dep[0] == 'eng' and dep[1] == eng and (dma is None):
                if kind != 'raw' or eng == 'pe':
                    return
            deps.add(dep)
        for r in reads:
            add(self.last_w.get(r), 'raw')
        for w in writes:
            add(self.last_w.get(w), 'waw')
            for d in self.readers.get(w, {}).values():
                add(d, 'war')
        if dma is None:
            me = ('eng', eng, idx)
        else:
            self.dma_cnt[dma] = self.dma_cnt.get(dma, 0) + 1
            me = ('dma', dma, self.dma_cnt[dma])
        for r in reads:
            self.readers.setdefault(r, {})[eng, dma] = me
        for w in writes:
            self.last_w[w] = me
            self.readers[w] = {}
        ops.append({'fn': fn, 'deps': deps, 'dma': dma, 'signal': False})
        return me

    def final_wait(self, eng, keys):
        deps = set()
        for k in keys:
            deps.add(('dma', k, self.dma_cnt[k]))
        self.ops[eng].append({'fn': None, 'deps': deps, 'dma': None, 'signal': False})

    def emit(self, nc, block, sems_eng, sems_dma):
        for e in self.ENG:
            for o in self.ops[e]:
                for d in o['deps']:
                    if d[0] == 'eng':
                        self.ops[d[1]][d[2]]['signal'] = True
        signo = {}
        for e in self.ENG:
            n = 0
            for i, o in enumerate(self.ops[e]):
                if o['signal']:
                    n += 1
                    signo[e, i] = n
        self.nsig = {e: sum((1 for o in self.ops[e] if o['signal'])) for e in self.ENG}

        def run(e, engine):
            waited = {}
            for i, o in enumerate(self.ops[e]):
                need = {}
                for d in o['deps']:
                    if d[0] == 'eng':
                        key = ('eng', d[1])
                        val = signo[d[1], d[2]]
                    else:
                        key = ('dma', d[1])
                        val = 16 * d[2]
                    if waited.get(key, 0) >= val:
                        continue
                    need[key] = max(need.get(key, 0), val)
                for key, val in need.items():
                    sem = sems_eng[key[1]] if key[0] == 'eng' else sems_dma[key[1]]
                    engine.wait_ge(sem, val)
                    waited[key] = val
                if o['fn'] is None:
                    continue
                ins = getattr(engine, o['fn'][0])(**o['fn'][1])
                if o['dma'] is not None:
                    ins.then_inc(sems_dma[o['dma']], 16)
                elif o['signal']:
                    ins.then_inc(sems_eng[e], 1)
        block.tensor(lambda eng: run('pe', eng))
        block.scalar(lambda eng: run('act', eng))
        block.vector(lambda eng: run('dve', eng))
        block.gpsimd(lambda eng: run('pool', eng))
        block.sync(lambda eng: run('sp', eng))

def build_nc(n_tiles=NSEQ * TPS, stage=99):
    nc = bass.Bass('TRN2', target_bir_lowering=False)
    NTOK = NSEQ * SEQ
    dram = {}

    def din(name, shape):
        dram[name] = nc.dram_tensor(name, list(shape), F32, kind='ExternalInput').ap()
        return dram[name]
    x = din('x', [NTOK, D])
    consts_d = din('consts', [P, NCONST])
    bv_d = din('bv_b', [P, 512])
    sinks_d = din('sinks_b', [P, 32])
    masks_d = din('masks', [P, 2 * P])
    ident_d = din('ident', [P, P])
    w_in = din('w_in', [D, 2 * D])
    w_out = din('w_out', [D, D])
    w_k = din('w_k', [D, 2 * 512])
    w_v = din('w_v', [D, 512])
    w_q = din('w_q', [D, D])
    w_o = din('w_o', [D, D])
    w_up = din('w_up', [2, D, DFF])
    w_down = din('w_down', [2, DFF, D])
    out = nc.dram_tensor('out', [NTOK, D], F32, kind='ExternalOutput').ap()
    S = Sched()
    from contextlib import ExitStack
    with ExitStack() as es:

        def sb(name, shape, dt):
            return es.enter_context(nc.sbuf_tensor('sb_' + name, list(shape), dt))
        h_t = sb('h', [P, KC, T], F32)
        xn_t = sb('xn', [P, KC, T], BF16)
        big_t = sb('big', [P, HC * T], BF16)
        wr_t = sb('wring', [P, NSLOT, KC, UW], BF16)
        consts_t = sb('consts', [P, NCONST], F32)
        bv_t = sb('bv', [P, 512], F32)
        sinks_t = sb('sinks', [P, 32], F32)
        esink_t = sb('esink', [P, 32], F32)
        masks_t = sb('masks', [P, 2 * P], BF16)
        ident_t = sb('ident', [P, P], F32)
        onesD_t = sb('onesD', [P, P], BF16)
        ones1_t = sb('ones1', [P, P], BF16)
        NSQ = 3
        sq_t = sb('sq', [P, NSQ, T], BF16)
        NSC = 3
        sc_t = sb('sc', [P, NSC, T], F32)
        NST = 4
        st_t = sb('st', [P, NST, T], F32)
        NUB = 3
        ub_t = sb('ub', [P, NUB, HALO + T], F32)
        halo_t = sb('halo', [P, KC, HALO], F32)
        kprev_t = sb('kprev', [P, NG, P], BF16)
        vprev_t = sb('vprev', [P, NG * P], BF16)
        NPT = 4
        pT_t = sb('pT', [P, NPT, T], BF16)
        psum = [es.enter_context(nc.psum_tensor('ps%d' % i, [P, T], F32)) for i in range(8)]
        big_bf = big_t.ap()
        big_f = big_t.bitcast(F32).ap()
        hid_v = big_bf.rearrange('p (c t) -> p c t', t=T)
        y_v = big_f[:, 0:KC * T].rearrange('p (c t) -> p c t', t=T)
        xst_v = big_f[:, 0:NTB * D].rearrange('p (b d) -> p b d', d=D)
        ost_v = big_f[:, NTB * D:2 * NTB * D].rearrange('p (b d) -> p b d', d=D)
        qT_v = big_bf[:, 0:KC * T].rearrange('p (c t) -> p c t', t=T)
        ao_v = big_bf[:, KC * T:2 * KC * T].rearrange('p (c t) -> p c t', t=T)
        KTW = (NTB + 1) * P
        kT_v = big_bf[:, 32 * T:32 * T + NG * KTW].rearrange('p (g k) -> p g k', k=KTW)
        vv_v = big_bf[:, 42 * T:42 * T + (NTB + 1) * NG * P].rearrange('p (b n) -> p b n', n=NG * P)

        def pg(lo, hi):
            return [('big', k) for k in range(lo, hi)]

        def C(name, col=0, rows=slice(0, P)):
            o = _c[name] + col
            return consts_t[rows, o:o + 1]
        bank_ctr = [0]

        def next_bank():
            b = bank_ctr[0] % 8
            bank_ctr[0] += 1
            return b
        rot = {'sq': 0, 'sc': 0, 'st': 0, 'ub': 0, 'pT': 0}

        def nxt(name, n):
            v = rot[name] % n
            rot[name] += 1
            return v
        wctr = [0]
        pending_units = []

        def wload(src_ap):
            slot = wctr[0] % NSLOT
            wctr[0] += 1
            S.op('pool', 'dma_start', dict(out=wr_t[:, slot], in_=src_ap.rearrange('(k p) n -> p k n', p=P)), writes=[('w', slot)], dma=('w', slot))
            return slot

        def mm(out_ap, lhsT, rhs, start, stop, reads, bank):
            S.op('pe', 'matmul', dict(out=out_ap, lhsT=lhsT, rhs=rhs, start=start, stop=stop), reads=reads, writes=[('ps', bank)])
        S.op('sp', 'dma_start', dict(out=consts_t[:], in_=consts_d), writes=['consts'], dma='c_consts')
        S.op('sp', 'dma_start', dict(out=bv_t[:], in_=bv_d), writes=['bv'], dma='c_bv')
        S.op('sp', 'dma_start', dict(out=sinks_t[:], in_=sinks_d), writes=['sinks'], dma='c_sinks')
        S.op('sp', 'dma_start', dict(out=ident_t[:], in_=ident_d), writes=['ident'], dma='c_ident')
        S.op('pool', 'dma_start', dict(out=masks_t[:], in_=masks_d), writes=['masks'], dma='c_masks')
        S.op('dve', 'memset', dict(ap=onesD_t[:], constant=1.0 / D), writes=['onesD'])
        S.op('dve', 'memset', dict(ap=ones1_t[:], constant=1.0), writes=['ones1'])
        S.op('act', 'activation', dict(out=esink_t[:], in_=sinks_t[:], func=AF.Exp), reads=['sinks'], writes=['esink'])

        def rmsnorm(gname, dst_reads_extra=()):
            bank = next_bank()
            for c in range(KC):
                q = nxt('sq', NSQ)
                S.op('act', 'activation', dict(out=sq_t[:, q], in_=h_t[:, c], func=AF.Square), reads=[('h', c)], writes=[('sq', q)])
                mm(psum[bank][:], onesD_t[:], sq_t[:, q], c == 0, c == KC - 1, reads=[('sq', q), 'onesD'], bank=bank)
            r = nxt('st', NST)
            S.op('act', 'activation', dict(out=st_t[:, r], in_=psum[bank][:], func=AF.Sqrt, bias=C('eps_n')), reads=[('ps', bank), 'consts'], writes=[('st', r)])
            S.op('dve', 'reciprocal', dict(out=st_t[:, r], in_=st_t[:, r]), reads=[('st', r)], writes=[('st', r)])
            return r

        def apply_norm(r, gname, dst, dst_res):
            for c in range(KC):
                S.op('dve', 'scalar_tensor_tensor', dict(out=dst[:, c], in0=h_t[:, c], scalar=C(gname, c), in1=st_t[:, r], op0=ALU.mult, op1=ALU.mult), reads=[('h', c), ('st', r), 'consts'], writes=[dst_res(c)])

        def proj_units(wsrc_fn, n_units):
            for u in range(n_units):
                yield (u, wload(wsrc_fn(u)))

        def mlp(layer):
            r = rmsnorm('mlp_norm%d' % layer)
            apply_norm(r, 'mlp_norm%d' % layer, xn_t, lambda c: ('xn', c))
            for u in range(DFF // UW):
                slot = wload(w_up[layer][:, u * UW:(u + 1) * UW])
                for m in range(UW // P):
                    hc = u * (UW // P) + m
                    bank = next_bank()
                    for kc in range(KC):
                        mm(psum[bank][:], wr_t[:, slot, kc, m * P:(m + 1) * P], xn_t[:, kc], kc == 0, kc == KC - 1, reads=[('w', slot), ('xn', kc)], bank=bank)
                    s = nxt('sc', NSC)
                    S.op('act', 'activation', dict(out=sc_t[:, s], in_=psum[bank][:], func=AF.Relu), reads=[('ps', bank)], writes=[('sc', s)])
                    S.op('dve', 'tensor_tensor', dict(out=hid_v[:, hc], in0=sc_t[:, s], in1=sc_t[:, s], op=ALU.mult), reads=[('sc', s)], writes=[('big', hc)])
            MPU = UW // P
            for ng in range(D // UW):
                banks = [next_bank() for _ in range(MPU)]
                for kg in range(DFF // D):
                    slot = wload(w_down[layer][kg * D:(kg + 1) * D, ng * UW:(ng + 1) * UW])
                    for m in range(MPU):
                        for kc in range(KC):
                            hc = kg * KC + kc
                            mm(psum[banks[m]][:], wr_t[:, slot, kc, m * P:(m + 1) * P], hid_v[:, hc], kg == 0 and kc == 0, kg == DFF // D - 1 and kc == KC - 1, reads=[('w', slot), ('big', hc)], bank=banks[m])
                for m in range(MPU):
                    oc = ng * MPU + m
                    S.op('dve', 'tensor_tensor', dict(out=h_t[:, oc], in0=psum[banks[m]][:], in1=h_t[:, oc], op=ALU.add), reads=[('ps', banks[m]), ('h', oc)], writes=[('h', oc)])
        for it in range(n_tiles):
            seq_first = it % TPS == 0
            tok0 = it * T
            S.op('sp', 'dma_start', dict(out=xst_v, in_=x[tok0:tok0 + T, :].rearrange('(b p) d -> p b d', p=P)), writes=pg(0, 32), dma='x')
            for c in range(KC):
                bank = next_bank()
                for tb in range(NTB):
                    S.op('pe', 'transpose', dict(out=psum[bank][:, tb * P:(tb + 1) * P], in_=xst_v[:, tb, c * P:(c + 1) * P], identity=ident_t[:]), reads=pg(8 * tb, 8 * tb + 8) + ['ident'], writes=[('ps', bank)])
                S.op('dve', 'tensor_copy', dict(out=h_t[:, c], in_=psum[bank][:]), reads=[('ps', bank)], writes=[('h', c)])
            if stage >= 1:
                r = rmsnorm('a_norm')
                apply_norm(r, 'a_norm', xn_t, lambda c: ('xn', c))
                if seq_first:
                    S.op('dve', 'memset', dict(ap=halo_t[:], constant=0.0), writes=[('halo', c) for c in range(KC)])
                for c in range(KC):
                    slot = wload(w_in[:, c * UW:(c + 1) * UW])
                    ba = next_bank()
                    bg = next_bank()
                    for kc in range(KC):
                        mm(psum[ba][:], wr_t[:, slot, kc, 0:P], xn_t[:, kc], kc == 0, kc == KC - 1, reads=[('w', slot), ('xn', kc)], bank=ba)
                    for kc in range(KC):
                        mm(psum[bg][:], wr_t[:, slot, kc, P:2 * P], xn_t[:, kc], kc == 0, kc == KC - 1, reads=[('w', slot), ('xn', kc)], bank=bg)
                    s = nxt('sc', NSC)
                    S.op('act', 'activation', dict(out=sc_t[:, s], in_=psum[bg][:], func=AF.Sigmoid, bias=C('b_in_g', c)), reads=[('ps', bg), 'consts'], writes=[('sc', s)])
                    ub = nxt('ub', NUB)
                    S.op('dve', 'tensor_copy', dict(out=ub_t[:, ub, 0:HALO], in_=halo_t[:, c]), reads=[('halo', c)], writes=[('ub', ub)])
                    S.op('dve', 'scalar_tensor_tensor', dict(out=ub_t[:, ub, HALO:HALO + T], in0=psum[ba][:], scalar=C('b_in_a', c), in1=sc_t[:, s], op0=ALU.add, op1=ALU.mult), reads=[('ps', ba), ('sc', s), 'consts'], writes=[('ub', ub)])
                    S.op('dve', 'tensor_copy', dict(out=halo_t[:, c], in_=ub_t[:, ub, T:T + HALO]), reads=[('ub', ub)], writes=[('halo', c)])
                    S.op('dve', 'tensor_scalar', dict(out=y_v[:, c], in0=ub_t[:, ub, 0:T], scalar1=C('w_dw', 0 * 16 + c), scalar2=C('b_dw', c), op0=ALU.mult, op1=ALU.add), reads=[('ub', ub), 'consts'], writes=pg(2 * c, 2 * c + 2))
                    for j in range(1, CW):
                        S.op('dve', 'scalar_tensor_tensor', dict(out=y_v[:, c], in0=ub_t[:, ub, j:j + T], scalar=C('w_dw', j * 16 + c), in1=y_v[:, c], op0=ALU.mult, op1=ALU.add), reads=[('ub', ub), 'consts'] + pg(2 * c, 2 * c + 2), writes=pg(2 * c, 2 * c + 2))
                bm = next_bank()
                bq = next_bank()
                for c in range(KC):
                    q1 = nxt('sq', NSQ)
                    S.op('act', 'activation', dict(out=sq_t[:, q1], in_=y_v[:, c], func=AF.Identity), reads=pg(2 * c, 2 * c + 2), writes=[('sq', q1)])
                    mm(psum[bm][:], onesD_t[:], sq_t[:, q1], c == 0, c == KC - 1, reads=[('sq', q1), 'onesD'], bank=bm)
                    q2 = nxt('sq', NSQ)
                    S.op('act', 'activation', dict(out=sq_t[:, q2], in_=y_v[:, c], func=AF.Square), reads=pg(2 * c, 2 * c + 2), writes=[('sq', q2)])
                    mm(psum[bq][:], onesD_t[:], sq_t[:, q2], c == 0, c == KC - 1, reads=[('sq', q2), 'onesD'], bank=bq)
                rm = nxt('st', NST)
                rv = nxt('st', NST)
                rn = nxt('st', NST)
                S.op('dve', 'tensor_copy', dict(out=st_t[:, rm], in_=psum[bm][:]), reads=[('ps', bm)], writes=[('st', rm)])
                S.op('dve', 'tensor_tensor', dict(out=st_t[:, rn], in0=st_t[:, rm], in1=st_t[:, rm], op=ALU.mult), reads=[('st', rm)], writes=[('st', rn)])
                S.op('dve', 'tensor_tensor', dict(out=st_t[:, rv], in0=psum[bq][:], in1=st_t[:, rn], op=ALU.subtract), reads=[('ps', bq), ('st', rn)], writes=[('st', rv)])
                S.op('act', 'activation', dict(out=st_t[:, rv], in_=st_t[:, rv], func=AF.Sqrt, bias=C('eps_ln')), reads=[('st', rv), 'consts'], writes=[('st', rv)])
                S.op('dve', 'reciprocal', dict(out=st_t[:, rv], in_=st_t[:, rv]), reads=[('st', rv)], writes=[('st', rv)])
                S.op('dve', 'scalar_tensor_tensor', dict(out=st_t[:, rn], in0=st_t[:, rm], scalar=-1.0, in1=st_t[:, rv], op0=ALU.mult, op1=ALU.mult), reads=[('st', rm), ('st', rv)], writes=[('st', rn)])
                for c in range(KC):
                    s = nxt('sc', NSC)
                    S.op('dve', 'tensor_tensor', dict(out=sc_t[:, s], in0=y_v[:, c], in1=st_t[:, rv], op=ALU.mult), reads=pg(2 * c, 2 * c + 2) + [('st', rv)], writes=[('sc', s)])
                    S.op('dve', 'tensor_tensor', dict(out=sc_t[:, s], in0=sc_t[:, s], in1=st_t[:, rn], op=ALU.add), reads=[('sc', s), ('st', rn)], writes=[('sc', s)])
                    S.op('act', 'activation', dict(out=xn_t[:, c], in_=sc_t[:, s], func=AF.Silu, bias=C('ln_b', c), scale=C('ln_g', c)), reads=[('sc', s), 'consts'], writes=[('xn', c)])
                MPU = UW // P
                for u in range(D // UW):
                    slot = wload(w_out[:, u * UW:(u + 1) * UW])
                    for m in range(MPU):
                        oc = u * MPU + m
                        bank = next_bank()
                        for kc in range(KC):
                            mm(psum[bank][:], wr_t[:, slot, kc, m * P:(m + 1) * P], xn_t[:, kc], kc == 0, kc == KC - 1, reads=[('w', slot), ('xn', kc)], bank=bank)
                        S.op('dve', 'scalar_tensor_tensor', dict(out=h_t[:, oc], in0=psum[bank][:], scalar=C('b_out', oc), in1=h_t[:, oc], op0=ALU.add, op1=ALU.add), reads=[('ps', bank), ('h', oc), 'consts'], writes=[('h', oc)])
            if stage >= 2:
                mlp(0)
            if stage >= 3:
                MPU = UW // P
                r = rmsnorm('kv_norm')
                apply_norm(r, 'kv_norm', xn_t, lambda c: ('xn', c))
                for u in range(2 * 512 // UW):
                    slot = wload(w_k[:, u * UW:(u + 1) * UW])
                    for m in range(MPU):
                        g = u * MPU + m
                        bank = next_bank()
                        for kc in range(KC):
                            mm(psum[bank][:], wr_t[:, slot, kc, m * P:(m + 1) * P], xn_t[:, kc], kc == 0, kc == KC - 1, reads=[('w', slot), ('xn', kc)], bank=bank)
                        S.op('act', 'activation', dict(out=kT_v[:, g, P:P + T], in_=psum[bank][:], func=AF.Identity, bias=C('b_k', g)), reads=[('ps', bank), 'consts'], writes=pg(32, 42))
                vslots = [wload(w_v[:, u * UW:(u + 1) * UW]) for u in range(512 // UW)]
                for tb in range(NTB):
                    bank = next_bank()
                    for u in range(512 // UW):
                        for kc in range(KC):
                            mm(psum[bank][:, u * UW:(u + 1) * UW], xn_t[:, kc, tb * P:(tb + 1) * P], wr_t[:, vslots[u], kc, :], kc == 0, kc == KC - 1, reads=[('w', vslots[u]), ('xn', kc)], bank=bank)
                    for dup in range(2):
                        S.op('dve', 'tensor_tensor', dict(out=vv_v[:, 1 + tb].rearrange('p (g u d) -> p g u d', u=2, d=64)[:, :, dup, :], in0=psum[bank][:].rearrange('p (g d) -> p g d', d=64), in1=bv_t[:].rearrange('p (g d) -> p g d', d=64), op=ALU.add), reads=[('ps', bank), 'bv'], writes=pg(42, 52))
                if not seq_first:
                    S.op('dve', 'tensor_copy', dict(out=kT_v[:, :, 0:P], in_=kprev_t[:]), reads=['kprev'], writes=pg(32, 42))
                    S.op('dve', 'tensor_copy', dict(out=vv_v[:, 0], in_=vprev_t[:]), reads=['vprev'], writes=pg(42, 52))
                r2 = r
                apply_norm(r2, 'b_norm', xn_t, lambda c: ('xn', c))
                for u in range(D // UW):
                    slot = wload(w_q[:, u * UW:(u + 1) * UW])
                    for m in range(MPU):
                        oc = u * MPU + m
                        bank = next_bank()
                        for kc in range(KC):
                            mm(psum[bank][:], wr_t[:, slot, kc, m * P:(m + 1) * P], xn_t[:, kc], kc == 0, kc == KC - 1, reads=[('w', slot), ('xn', kc)], bank=bank)
                        S.op('act', 'activation', dict(out=qT_v[:, oc], in_=psum[bank][:], func=AF.Identity, bias=C('b_q', oc)), reads=[('ps', bank), 'consts'], writes=[('big', oc)])
                for n in range(NTB):
                    for g in range(NG):
                        kbs = [1] if seq_first and n == 0 else [0, 1]
                        c0 = kbs[0] * 2 * P
                        sb_ = [next_bank(), next_bank()]
                        pts = [nxt('pT', NPT), nxt('pT', NPT)]
                        for kb in kbs:
                            for a in range(2):
                                for par in range(2):
                                    ch = 2 * g + a
                                    rows = slice(par * 64, par * 64 + 64)
                                    col = (kb * 2 + a) * P
                                    mm(psum[sb_[par]][:, col:col + P], kT_v[rows, g, (n + kb) * P:(n + kb + 1) * P], qT_v[rows, ch, n * P:(n + 1) * P], True, True, reads=pg(32, 42) + [('big', ch)], bank=sb_[par])
                        for par in range(2):
                            pt = pts[par]
                            S.op('act', 'activation', dict(out=pT_t[:, pt, c0:T], in_=psum[sb_[par]][:, c0:T], func=AF.Exp, scale=0.125), reads=[('ps', sb_[par])], writes=[('pT', pt)])
                            nk = len(kbs)
                            S.op('dve', 'tensor_tensor', dict(out=pT_t[:, pt, c0:T].rearrange('p (k a q) -> p k a q', a=2, q=P), in0=pT_t[:, pt, c0:T].rearrange('p (k a q) -> p k a q', a=2, q=P), in1=masks_t[:, kbs[0] * P:2 * P].rearrange('p (k q) -> p k q', q=P).unsqueeze(2).broadcast_to([P, nk, 2, P]), op=ALU.mult), reads=[('pT', pt), 'masks'], writes=[('pT', pt)])
                        bo = next_bank()
                        bd = next_bank()
                        for par in range(2):
                            for i, kb in enumerate(kbs):
                                mm(psum[bo][:, par * 2 * P:(par + 1) * 2 * P], vv_v[:, n + kb, g * P:(g + 1) * P], pT_t[:, pts[par], kb * 2 * P:(kb + 1) * 2 * P], i == 0, i == len(kbs) - 1, reads=pg(42, 52) + [('pT', pts[par])], bank=bo)
                        for par in range(2):
                            for i, kb in enumerate(kbs):
                                mm(psum[bd][:, par * 2 * P:(par + 1) * 2 * P], ones1_t[:], pT_t[:, pts[par], kb * 2 * P:(kb + 1) * 2 * P], i == 0, i == len(kbs) - 1, reads=['ones1', ('pT', pts[par])], bank=bd)
                        s = nxt('sc', NSC)
                        S.op('dve', 'tensor_tensor', dict(out=sc_t[:, s].rearrange('p (b a q) -> p b a q', a=2, q=P), in0=psum[bd][:].rearrange('p (b a q) -> p b a q', a=2, q=P), in1=esink_t[:, 4 * g:4 * g + 4].rearrange('p (a b) -> p b a', b=2).unsqueeze(3).broadcast_to([P, 2, 2, P]), op=ALU.add), reads=[('ps', bd), 'esink'], writes=[('sc', s)])
                        S.op('dve', 'reciprocal', dict(out=sc_t[:, s], in_=sc_t[:, s]), reads=[('sc', s)], writes=[('sc', s)])
                        for par in range(2):
                            rows = slice(par * 64, par * 64 + 64)
                            S.op('dve', 'tensor_tensor', dict(out=ao_v[rows, 2 * g:2 * g + 2, n * P:(n + 1) * P], in0=psum[bo][rows, par * 2 * P:(par + 1) * 2 * P].rearrange('p (a q) -> p a q', q=P), in1=sc_t[rows, s, par * 2 * P:(par + 1) * 2 * P].rearrange('p (a q) -> p a q', q=P), op=ALU.mult), reads=[('ps', bo), ('sc', s)], writes=[('big', 16 + 2 * g), ('big', 16 + 2 * g + 1)])
                S.op('dve', 'tensor_copy', dict(out=kprev_t[:], in_=kT_v[:, :, NTB * P:(NTB + 1) * P]), reads=pg(32, 42), writes=['kprev'])
                S.op('dve', 'tensor_copy', dict(out=vprev_t[:], in_=vv_v[:, NTB]), reads=pg(42, 52), writes=['vprev'])
                for u in range(D // UW):
                    slot = wload(w_o[:, u * UW:(u + 1) * UW])
                    for m in range(MPU):
                        oc = u * MPU + m
                        bank = next_bank()
                        for kc in range(KC):
                            mm(psum[bank][:], wr_t[:, slot, kc, m * P:(m + 1) * P], ao_v[:, kc], kc == 0, kc == KC - 1, reads=[('w', slot), ('big', 16 + kc)], bank=bank)
                        S.op('dve', 'scalar_tensor_tensor', dict(out=h_t[:, oc], in0=psum[bank][:], scalar=C('b_o', oc), in1=h_t[:, oc], op0=ALU.add, op1=ALU.add), reads=[('ps', bank), ('h', oc), 'consts'], writes=[('h', oc)])
            if stage >= 4:
                mlp(1)
            if stage >= 5:
                r = rmsnorm('final_norm')
            for c in range(KC):
                s = nxt('sc', NSC)
                if stage >= 5:
                    S.op('dve', 'scalar_tensor_tensor', dict(out=sc_t[:, s], in0=h_t[:, c], scalar=C('final_norm', c), in1=st_t[:, r], op0=ALU.mult, op1=ALU.mult), reads=[('h', c), ('st', r), 'consts'], writes=[('sc', s)])
                else:
                    S.op('dve', 'tensor_copy', dict(out=sc_t[:, s], in_=h_t[:, c]), reads=[('h', c)], writes=[('sc', s)])
                bank = next_bank()
                for tb in range(NTB):
                    S.op('pe', 'transpose', dict(out=psum[bank][:, tb * P:(tb + 1) * P], in_=sc_t[:, s, tb * P:(tb + 1) * P], identity=ident_t[:]), reads=[('sc', s), 'ident'], writes=[('ps', bank)])
                S.op('act', 'activation', dict(out=ost_v[:, :, c * P:(c + 1) * P], in_=psum[bank][:].rearrange('p (b q) -> p b q', q=P), func=AF.Identity), reads=[('ps', bank)], writes=pg(32, 64))
            S.op('sp', 'dma_start', dict(out=out[tok0:tok0 + T, :].rearrange('(b p) d -> p b d', p=P), in_=ost_v), reads=pg(32, 64), dma='o')
        S.final_wait('sp', ['o'])
        dma_keys = sorted(set((o['dma'] for e in S.ENG for o in S.ops[e] if o['dma'] is not None)), key=str)
        sems_eng = {e: es.enter_context(nc.semaphore('s_' + e)) for e in S.ENG}
        sems_dma = {k: es.enter_context(nc.semaphore('d_%d' % i)) for i, k in enumerate(dma_keys)}
        block = es.enter_context(nc.Block())
        S.emit(nc, block, sems_eng, sems_dma)
    return nc

def _col(v):
    v = np.asarray(v, np.float32)
    return np.ascontiguousarray(v.reshape(-1, P).T)

def prep_shared(inp):
    f = lambda k: np.asarray(inp[k], np.float32)
    consts = np.zeros((P, NCONST), np.float32)

    def put(name, arr):
        consts[:, _c[name]:_c[name] + arr.shape[1]] = arr
    put('a_norm', _col(f('a_norm')[0]))
    b_in = f('a_b_in')[0]
    put('b_in_a', _col(b_in[:D]))
    put('b_in_g', _col(b_in[D:]))
    wdw = f('a_w_dw')[0]
    put('w_dw', np.concatenate([_col(wdw[j]) for j in range(CW)], axis=1))
    put('b_dw', _col(f('a_b_dw')[0]))
    put('ln_g', _col(f('a_ln_g')[0]))
    put('ln_b', _col(f('a_ln_b')[0]))
    put('b_out', _col(f('a_b_out')[0]))
    put('kv_norm', _col(f('kv_norm')))
    bk = f('b_k').reshape(NG, 1, 64)
    put('b_k', _col(np.repeat(bk, 2, axis=1).reshape(-1)))
    put('b_norm', _col(f('b_norm')[0]))
    put('b_q', _col(f('b_b_q')[0]))
    put('b_o', _col(f('b_b_o')[0]))
    put('mlp_norm0', _col(f('mlp_norm')[0]))
    put('mlp_norm1', _col(f('mlp_norm')[1]))
    put('final_norm', _col(f('final_norm')))
    put('eps_n', np.full((P, 1), NORM_EPS, np.float32))
    put('eps_ln', np.full((P, 1), LN_EPS, np.float32))
    w_in = f('a_w_in')[0]
    a_half = w_in[:, :D].reshape(D, KC, 1, P)
    g_half = w_in[:, D:].reshape(D, KC, 1, P)
    w_in_p = np.ascontiguousarray(np.concatenate([a_half, g_half], axis=2).reshape(D, 2 * D))
    wk = f('w_k').reshape(D, NG, 1, 64)
    wk_dup = np.ascontiguousarray(np.repeat(wk, 2, axis=2).reshape(D, 2 * 512))
    kk = np.arange(P)[:, None]
    qq = np.arange(P)[None, :]
    masks = np.concatenate([kk > qq, kk <= qq], axis=1).astype(np.float32)
    return {'consts': consts, 'bv_b': np.ascontiguousarray(np.broadcast_to(f('b_v')[None, :], (P, 512))), 'sinks_b': np.ascontiguousarray(np.broadcast_to(f('b_sinks')[0][None, :], (P, 32))), 'masks': masks, 'ident': np.eye(P, dtype=np.float32), 'w_in': w_in_p, 'w_out': np.ascontiguousarray(f('a_w_out')[0]), 'w_k': wk_dup, 'w_v': np.ascontiguousarray(f('w_v')), 'w_q': np.ascontiguousarray(f('b_w_q')[0]), 'w_o': np.ascontiguousarray(f('b_w_o')[0]), 'w_up': np.ascontiguousarray(f('mlp_w_up')), 'w_down': np.ascontiguousarray(f('mlp_w_down'))}

def kernel(**inputs):
    n_cores = 8
    x = np.asarray(inputs['x'], np.float32)
    B = x.shape[0]
    shared = prep_shared(inputs)
    nc = build_nc()
    in_maps = []
    for c in range(n_cores):
        m = dict(shared)
        m['x'] = np.ascontiguousarray(x[c * NSEQ:(c + 1) * NSEQ].reshape(NSEQ * SEQ, D))
        in_maps.append(m)
    res = run_bass_kernel_spmd(nc, in_maps, core_ids=list(range(n_cores)))
    outs = [np.asarray(r['out'], np.float32).reshape(NSEQ, SEQ, D) for r in res.results]
    return np.concatenate(outs, axis=0).reshape(B, SEQ, D)
```

```python
import numpy as np
import concourse.bass as bass
import concourse.mybir as mybir
from concourse.bass_utils import run_bass_kernel_spmd
F32 = mybir.dt.float32
BF16 = mybir.dt.bfloat16
AF = mybir.ActivationFunctionType
ALU = mybir.AluOpType
P = 128
D = 2048
KC = D // P
T = 512
NTB = T // P
SEQ = 2048
NSEQ = 2
TPS = SEQ // T
DFF = 8192
HC = DFF // P
CW = 31
HALO = CW - 1
NG = 8
UW = 256
NSLOT = 4
NORM_EPS = 1e-06
LN_EPS = 1e-05
ND_DVE = 10
_c = {}
_off = 0
for _name, _n in [('a_norm', 16), ('b_in_a', 16), ('b_in_g', 16), ('w_dw', CW * 16), ('b_dw', 16), ('ln_g', 16), ('ln_b', 16), ('b_out', 16), ('kv_norm', 16), ('b_k', 8), ('b_norm', 16), ('b_q', 16), ('b_o', 16), ('mlp_norm0', 16), ('mlp_norm1', 16), ('final_norm', 16), ('eps_n', 1), ('eps_ln', 1)]:
    _c[_name] = _off
    _off += _n
NCONST = _off

class Sched:
    ENG = ('pe', 'act', 'dve', 'pool', 'sp')

    def __init__(self):
        self.ops = {e: [] for e in self.ENG}
        self.last_w = {}
        self.readers = {}
        self.dma_cnt = {}

    def op(self, eng, name, kw, reads=(), writes=(), dma=None):
        fn = (name, kw)
        ops = self.ops[eng]
        idx = len(ops)
        deps = set()

        def add(dep, kind):
            if dep is None:
                return
            if dep[0] == 'eng' and dep[1] == eng and (dma is None):
                if kind != 'raw' or eng == 'pe':
                    return
            deps.add(dep)
        for r in reads:
            add(self.last_w.get(r), 'raw')
        for w in writes:
            add(self.last_w.get(w), 'waw')
            for d in self.readers.get(w, {}).values():
                add(d, 'war')
        if dma is None:
            me = ('eng', eng, idx)
        else:
            self.dma_cnt[dma] = self.dma_cnt.get(dma, 0) + 1
            me = ('dma', dma, self.dma_cnt[dma])
        for r in reads:
            self.readers.setdefault(r, {})[eng, dma] = me
        for w in writes:
            self.last_w[w] = me
            self.readers[w] = {}
        ops.append({'fn': fn, 'deps': deps, 'dma': dma, 'signal': False})
        return me

    def final_wait(self, eng, keys):
        deps = set()
        for k in keys:
            deps.add(('dma', k, self.dma_cnt[k]))
        self.ops[eng].append({'fn': None, 'deps': deps, 'dma': None, 'signal': False})

    def emit(self, nc, block, sems_eng, sems_dma):
        for e in self.ENG:
            for o in self.ops[e]:
                for d in o['deps']:
                    if d[0] == 'eng':
                        self.ops[d[1]][d[2]]['signal'] = True
        signo = {}
        for e in self.ENG:
            n = 0
            for i, o in enumerate(self.ops[e]):
                if o['signal']:
                    n += 1
                    signo[e, i] = n
        self.nsig = {e: sum((1 for o in self.ops[e] if o['signal'])) for e in self.ENG}

        def run(e, engine):
            waited = {}
            for i, o in enumerate(self.ops[e]):
                need = {}
                for d in o['deps']:
                    if d[0] == 'eng':
                        key = ('eng', d[1])
                        val = signo[d[1], d[2]]
                    else:
                        key = ('dma', d[1])
                        val = 16 * d[2]
                    if waited.get(key, 0) >= val:
                        continue
                    need[key] = max(need.get(key, 0), val)
                for key, val in need.items():
                    sem = sems_eng[key[1]] if key[0] == 'eng' else sems_dma[key[1]]
                    engine.wait_ge(sem, val)
                    waited[key] = val
                if o['fn'] is None:
                    continue
                ins = getattr(engine, o['fn'][0])(**o['fn'][1])
                if o['dma'] is not None:
                    ins.then_inc(sems_dma[o['dma']], 16)
                elif o['signal']:
                    ins.then_inc(sems_eng[e], 1)
        block.tensor(lambda eng: run('pe', eng))
        block.scalar(lambda eng: run('act', eng))
        block.vector(lambda eng: run('dve', eng))
        block.gpsimd(lambda eng: run('pool', eng))
        block.sync(lambda eng: run('sp', eng))

def build_nc(n_tiles=NSEQ * TPS, stage=99):
    nc = bass.Bass('TRN2', target_bir_lowering=False)
    NTOK = NSEQ * SEQ
    dram = {}

    def din(name, shape):
        dram[name] = nc.dram_tensor(name, list(shape), F32, kind='ExternalInput').ap()
        return dram[name]
    x = din('x', [NTOK, D])
    consts_d = din('consts', [P, NCONST])
    bv_d = din('bv_b', [P, 512])
    sinks_d = din('sinks_b', [P, 32])
    masks_d = din('masks', [P, 2 * P])
    ident_d = din('ident', [P, P])
    w_in = din('w_in', [D, 2 * D])
    dwdiag = din('dwdiag', [KC, D, UW])
    w_out = din('w_out', [D, D])
    w_k = din('w_k', [D, 2 * 512])
    w_v = din('w_v', [D, 512])
    w_q = din('w_q', [D, D])
    w_o = din('w_o', [D, D])
    w_up = din('w_up', [2, D, DFF])
    w_down = din('w_down', [2, DFF, D])
    out = nc.dram_tensor('out', [NTOK, D], F32, kind='ExternalOutput').ap()
    S = Sched()
    from contextlib import ExitStack
    with ExitStack() as es:

        def sb(name, shape, dt):
            return es.enter_context(nc.sbuf_tensor('sb_' + name, list(shape), dt))
        h_t = sb('h', [P, KC, T], F32)
        xn_t = sb('xn', [P, KC, T], BF16)
        big_t = sb('big', [P, HC * T], BF16)
        wr_t = sb('wring', [P, NSLOT, KC, UW], BF16)
        consts_t = sb('consts', [P, NCONST], F32)
        bv_t = sb('bv', [P, 512], F32)
        sinks_t = sb('sinks', [P, 32], F32)
        esink_t = sb('esink', [P, 32], F32)
        masks_t = sb('masks', [P, 2 * P], BF16)
        ident_t = sb('ident', [P, P], F32)
        onesD_t = sb('onesD', [P, P], BF16)
        ones1_t = sb('ones1', [P, P], BF16)
        NSQ = 3
        sq_t = sb('sq', [P, NSQ, T], BF16)
        NSC = 5
        sc_t = sb('sc', [P, NSC, T], F32)
        NST = 4
        st_t = sb('st', [P, NST, T], F32)
        NUB = 3
        NYP = 2
        yp_t = sb('yp', [P, NYP, T], F32)
        ub_t = sb('ub', [P, NUB, HALO + T], BF16)
        halo_t = sb('halo', [P, KC, HALO], BF16)
        kprev_t = sb('kprev', [P, NG, P], BF16)
        vprev_t = sb('vprev', [P, NG * P], BF16)
        NPT = 8
        pT_t = sb('pT', [P, NPT, T], BF16)
        psum = [es.enter_context(nc.psum_tensor('ps%d' % i, [P, T], F32)) for i in range(8)]
        big_bf = big_t.ap()
        big_f = big_t.bitcast(F32).ap()
        hid_v = big_bf.rearrange('p (c t) -> p c t', t=T)
        y_v = big_f[:, 0:KC * T].rearrange('p (c t) -> p c t', t=T)
        xst_v = big_f[:, 0:NTB * D].rearrange('p (b d) -> p b d', d=D)
        ost_v = big_f[:, NTB * D:2 * NTB * D].rearrange('p (b d) -> p b d', d=D)
        qT_v = big_bf[:, 0:KC * T].rearrange('p (c t) -> p c t', t=T)
        ao_v = big_bf[:, KC * T:2 * KC * T].rearrange('p (c t) -> p c t', t=T)
        KTW = (NTB + 1) * P
        kT_v = big_bf[:, 32 * T:32 * T + NG * KTW].rearrange('p (g k) -> p g k', k=KTW)
        vv_v = big_bf[:, 42 * T:42 * T + (NTB + 1) * NG * P].rearrange('p (b n) -> p b n', n=NG * P)

        def pg(lo, hi):
            return [('big', k) for k in range(lo, hi)]

        def C(name, col=0, rows=slice(0, P)):
            o = _c[name] + col
            return consts_t[rows, o:o + 1]
        bank_ctr = [0]

        reserved = set()

        def next_bank():
            while True:
                b = bank_ctr[0] % 8
                bank_ctr[0] += 1
                if b not in reserved:
                    return b
        rot = {'sq': 0, 'sc': 0, 'st': 0, 'ub': 0, 'pT': 0, 'yp': 0}

        def nxt(name, n):
            v = rot[name] % n
            rot[name] += 1
            return v
        wctr = [0]
        pending_units = []

        NUNITS = 190
        wcache = nc.dram_tensor('wcache', [NUNITS, P, KC * UW], BF16).ap()
        ucnt = [0]
        cur_tile = [0]

        def wload(src_ap):
            slot = wctr[0] % NSLOT
            wctr[0] += 1
            uid = ucnt[0]
            ucnt[0] += 1
            assert uid < NUNITS
            cview = wcache[uid].rearrange('p (k n) -> p k n', n=UW)
            if cur_tile[0] == 0:
                S.op('pool', 'dma_start', dict(out=wr_t[:, slot], in_=src_ap.rearrange('(k p) n -> p k n', p=P)), writes=[('w', slot)], dma=('w', slot))
                S.op('sp', 'dma_start', dict(out=cview, in_=wr_t[:, slot]), reads=[('w', slot)], writes=[('wc', uid)], dma=('wb', slot))
            else:
                S.op('sp', 'dma_start', dict(out=wr_t[:, slot], in_=cview), reads=[('wc', uid)], writes=[('w', slot)], dma=('wh', slot))
            return slot

        def mm(out_ap, lhsT, rhs, start, stop, reads, bank):
            S.op('pe', 'matmul', dict(out=out_ap, lhsT=lhsT, rhs=rhs, start=start, stop=stop), reads=reads, writes=[('ps', bank)])
        S.op('sp', 'dma_start', dict(out=consts_t[:], in_=consts_d), writes=['consts'], dma='c_consts')
        S.op('sp', 'dma_start', dict(out=bv_t[:], in_=bv_d), writes=['bv'], dma='c_bv')
        S.op('sp', 'dma_start', dict(out=sinks_t[:], in_=sinks_d), writes=['sinks'], dma='c_sinks')
        S.op('sp', 'dma_start', dict(out=ident_t[:], in_=ident_d), writes=['ident'], dma='c_ident')
        S.op('pool', 'dma_start', dict(out=masks_t[:], in_=masks_d), writes=['masks'], dma='c_masks')
        S.op('dve', 'memset', dict(ap=onesD_t[:], constant=1.0 / D), writes=['onesD'])
        S.op('dve', 'memset', dict(ap=ones1_t[:], constant=1.0), writes=['ones1'])
        S.op('act', 'activation', dict(out=esink_t[:], in_=sinks_t[:], func=AF.Exp), reads=['sinks'], writes=['esink'])

        def rmsnorm(gname, dst_reads_extra=()):
            bank = next_bank()
            for c in range(KC):
                q = nxt('sq', NSQ)
                S.op('act', 'activation', dict(out=sq_t[:, q], in_=h_t[:, c], func=AF.Square), reads=[('h', c)], writes=[('sq', q)])
                mm(psum[bank][:], onesD_t[:], sq_t[:, q], c == 0, c == KC - 1, reads=[('sq', q), 'onesD'], bank=bank)
            r = nxt('st', NST)
            S.op('act', 'activation', dict(out=st_t[:, r], in_=psum[bank][:], func=AF.Ln, bias=C('eps_n')), reads=[('ps', bank), 'consts'], writes=[('st', r)])
            S.op('act', 'activation', dict(out=st_t[:, r], in_=st_t[:, r], func=AF.Exp, scale=-0.5), reads=[('st', r)], writes=[('st', r)])
            return r

        def apply_norm(r, gname, dst, dst_res):
            for c in range(KC):
                S.op('dve', 'scalar_tensor_tensor', dict(out=dst[:, c], in0=h_t[:, c], scalar=C(gname, c), in1=st_t[:, r], op0=ALU.mult, op1=ALU.mult), reads=[('h', c), ('st', r), 'consts'], writes=[dst_res(c)])

        def proj_units(wsrc_fn, n_units):
            for u in range(n_units):
                yield (u, wload(wsrc_fn(u)))

        def mlp(layer):
            r = rmsnorm('mlp_norm%d' % layer)
            apply_norm(r, 'mlp_norm%d' % layer, xn_t, lambda c: ('xn', c))
            for u in range(DFF // UW):
                slot = wload(w_up[layer][:, u * UW:(u + 1) * UW])
                for m in range(UW // P):
                    hc = u * (UW // P) + m
                    bank = next_bank()
                    for kc in range(KC):
                        mm(psum[bank][:], wr_t[:, slot, kc, m * P:(m + 1) * P], xn_t[:, kc], kc == 0, kc == KC - 1, reads=[('w', slot), ('xn', kc)], bank=bank)
                    s = nxt('sc', NSC)
                    S.op('act', 'activation', dict(out=sc_t[:, s], in_=psum[bank][:], func=AF.Relu), reads=[('ps', bank)], writes=[('sc', s)])
                    S.op('dve', 'tensor_tensor', dict(out=hid_v[:, hc], in0=sc_t[:, s], in1=sc_t[:, s], op=ALU.mult), reads=[('sc', s)], writes=[('big', hc)])
            MPU = UW // P
            for ng in range(D // UW):
                banks = [next_bank() for _ in range(MPU)]
                for kg in range(DFF // D):
                    slot = wload(w_down[layer][kg * D:(kg + 1) * D, ng * UW:(ng + 1) * UW])
                    for m in range(MPU):
                        for kc in range(KC):
                            hc = kg * KC + kc
                            mm(psum[banks[m]][:], wr_t[:, slot, kc, m * P:(m + 1) * P], hid_v[:, hc], kg == 0 and kc == 0, kg == DFF // D - 1 and kc == KC - 1, reads=[('w', slot), ('big', hc)], bank=banks[m])
                for m in range(MPU):
                    oc = ng * MPU + m
                    S.op('dve', 'tensor_tensor', dict(out=h_t[:, oc], in0=psum[banks[m]][:], in1=h_t[:, oc], op=ALU.add), reads=[('ps', banks[m]), ('h', oc)], writes=[('h', oc)])
        for it in range(n_tiles):
            seq_first = it % TPS == 0
            cur_tile[0] = it
            ucnt[0] = 0
            tok0 = it * T
            S.op('sp', 'dma_start', dict(out=xst_v, in_=x[tok0:tok0 + T, :].rearrange('(b p) d -> p b d', p=P)), writes=pg(0, 32), dma='x')
            for c in range(KC):
                bank = next_bank()
                for tb in range(NTB):
                    S.op('pe', 'transpose', dict(out=psum[bank][:, tb * P:(tb + 1) * P], in_=xst_v[:, tb, c * P:(c + 1) * P], identity=ident_t[:]), reads=pg(8 * tb, 8 * tb + 8) + ['ident'], writes=[('ps', bank)])
                S.op('dve', 'tensor_copy', dict(out=h_t[:, c], in_=psum[bank][:]), reads=[('ps', bank)], writes=[('h', c)])
            if stage >= 1:
                r = rmsnorm('a_norm')
                apply_norm(r, 'a_norm', xn_t, lambda c: ('xn', c))
                if seq_first:
                    S.op('dve', 'memset', dict(ap=halo_t[:], constant=0.0), writes=[('halo', c) for c in range(KC)])
                bm = next_bank()
                bq = next_bank()
                reserved.update([bm, bq])
                pend = []

                def ln_stats(c, first, last):
                    q1 = nxt('sq', NSQ)
                    S.op('act', 'activation', dict(out=sq_t[:, q1], in_=y_v[:, c], func=AF.Identity), reads=pg(2 * c, 2 * c + 2), writes=[('sq', q1)])
                    mm(psum[bm][:], onesD_t[:], sq_t[:, q1], first, last, reads=[('sq', q1), 'onesD'], bank=bm)
                    q2 = nxt('sq', NSQ)
                    S.op('act', 'activation', dict(out=sq_t[:, q2], in_=y_v[:, c], func=AF.Square), reads=pg(2 * c, 2 * c + 2), writes=[('sq', q2)])
                    mm(psum[bq][:], onesD_t[:], sq_t[:, q2], first, last, reads=[('sq', q2), 'onesD'], bank=bq)

                for c in range(KC):
                    slot = wload(w_in[:, c * UW:(c + 1) * UW])
                    slot_d = wload(dwdiag[c])
                    ba = next_bank()
                    bg = next_bank()
                    for kc in range(KC):
                        mm(psum[ba][:], wr_t[:, slot, kc, 0:P], xn_t[:, kc], kc == 0, kc == KC - 1, reads=[('w', slot), ('xn', kc)], bank=ba)
                    for kc in range(KC):
                        mm(psum[bg][:], wr_t[:, slot, kc, P:2 * P], xn_t[:, kc], kc == 0, kc == KC - 1, reads=[('w', slot), ('xn', kc)], bank=bg)
                    s = nxt('sc', NSC)
                    S.op('act', 'activation', dict(out=sc_t[:, s], in_=psum[bg][:], func=AF.Sigmoid, bias=C('b_in_g', c)), reads=[('ps', bg), 'consts'], writes=[('sc', s)])
                    ub = nxt('ub', NUB)
                    S.op('dve', 'tensor_copy', dict(out=ub_t[:, ub, 0:HALO], in_=halo_t[:, c]), reads=[('halo', c)], writes=[('ub', ub)])
                    S.op('dve', 'scalar_tensor_tensor', dict(out=ub_t[:, ub, HALO:HALO + T], in0=psum[ba][:], scalar=C('b_in_a', c), in1=sc_t[:, s], op0=ALU.add, op1=ALU.mult), reads=[('ps', ba), ('sc', s), 'consts'], writes=[('ub', ub)])
                    S.op('dve', 'tensor_copy', dict(out=halo_t[:, c], in_=ub_t[:, ub, T:T + HALO]), reads=[('ub', ub)], writes=[('halo', c)])
                    if len(pend) > 1:
                        pc = pend.pop(0)
                        ln_stats(pc, pc == 0, False)
                    bc = next_bank()
                    NPE = CW - ND_DVE
                    for j in range(NPE):
                        mm(psum[bc][:], wr_t[:, slot_d, j // 2, (j % 2) * P:(j % 2 + 1) * P], ub_t[:, ub, j:j + T], j == 0, j == NPE - 1, reads=[('w', slot_d), ('ub', ub)], bank=bc)
                    yp = nxt('yp', NYP)
                    for j in range(NPE, CW):
                        if j == NPE:
                            S.op('dve', 'tensor_scalar', dict(out=yp_t[:, yp], in0=ub_t[:, ub, j:j + T], scalar1=C('w_dw', j * 16 + c), scalar2=None, op0=ALU.mult), reads=[('ub', ub), 'consts'], writes=[('yp', yp)])
                        else:
                            S.op('dve', 'scalar_tensor_tensor', dict(out=yp_t[:, yp], in0=ub_t[:, ub, j:j + T], scalar=C('w_dw', j * 16 + c), in1=yp_t[:, yp], op0=ALU.mult, op1=ALU.add), reads=[('ub', ub), ('yp', yp), 'consts'], writes=[('yp', yp)])
                    S.op('dve', 'scalar_tensor_tensor', dict(out=y_v[:, c], in0=psum[bc][:], scalar=C('b_dw', c), in1=yp_t[:, yp], op0=ALU.add, op1=ALU.add), reads=[('ps', bc), ('yp', yp), 'consts'], writes=pg(2 * c, 2 * c + 2))
                    pend.append(c)
                while pend:
                    pc = pend.pop(0)
                    ln_stats(pc, pc == 0, pc == KC - 1)
                reserved.clear()
                rm = nxt('st', NST)
                rv = nxt('st', NST)
                rn = nxt('st', NST)
                S.op('dve', 'tensor_copy', dict(out=st_t[:, rm], in_=psum[bm][:]), reads=[('ps', bm)], writes=[('st', rm)])
                S.op('dve', 'tensor_tensor', dict(out=st_t[:, rn], in0=st_t[:, rm], in1=st_t[:, rm], op=ALU.mult), reads=[('st', rm)], writes=[('st', rn)])
                S.op('dve', 'tensor_tensor', dict(out=st_t[:, rv], in0=psum[bq][:], in1=st_t[:, rn], op=ALU.subtract), reads=[('ps', bq), ('st', rn)], writes=[('st', rv)])
                S.op('act', 'activation', dict(out=st_t[:, rv], in_=st_t[:, rv], func=AF.Ln, bias=C('eps_ln')), reads=[('st', rv), 'consts'], writes=[('st', rv)])
                S.op('act', 'activation', dict(out=st_t[:, rv], in_=st_t[:, rv], func=AF.Exp, scale=-0.5), reads=[('st', rv)], writes=[('st', rv)])
                S.op('dve', 'scalar_tensor_tensor', dict(out=st_t[:, rn], in0=st_t[:, rm], scalar=-1.0, in1=st_t[:, rv], op0=ALU.mult, op1=ALU.mult), reads=[('st', rm), ('st', rv)], writes=[('st', rn)])
                for c in range(KC):
                    s = nxt('sc', NSC)
                    S.op('dve', 'tensor_tensor', dict(out=sc_t[:, s], in0=y_v[:, c], in1=st_t[:, rv], op=ALU.mult), reads=pg(2 * c, 2 * c + 2) + [('st', rv)], writes=[('sc', s)])
                    S.op('dve', 'tensor_tensor', dict(out=sc_t[:, s], in0=sc_t[:, s], in1=st_t[:, rn], op=ALU.add), reads=[('sc', s), ('st', rn)], writes=[('sc', s)])
                    S.op('act', 'activation', dict(out=xn_t[:, c], in_=sc_t[:, s], func=AF.Silu, bias=C('ln_b', c), scale=C('ln_g', c)), reads=[('sc', s), 'consts'], writes=[('xn', c)])
                MPU = UW // P
                for u in range(D // UW):
                    slot = wload(w_out[:, u * UW:(u + 1) * UW])
                    for m in range(MPU):
                        oc = u * MPU + m
                        bank = next_bank()
                        for kc in range(KC):
                            mm(psum[bank][:], wr_t[:, slot, kc, m * P:(m + 1) * P], xn_t[:, kc], kc == 0, kc == KC - 1, reads=[('w', slot), ('xn', kc)], bank=bank)
                        S.op('dve', 'scalar_tensor_tensor', dict(out=h_t[:, oc], in0=psum[bank][:], scalar=C('b_out', oc), in1=h_t[:, oc], op0=ALU.add, op1=ALU.add), reads=[('ps', bank), ('h', oc), 'consts'], writes=[('h', oc)])
            if stage >= 2:
                mlp(0)
            if stage >= 3:
                MPU = UW // P
                r = rmsnorm('kv_norm')
                apply_norm(r, 'kv_norm', xn_t, lambda c: ('xn', c))
                for u in range(2 * 512 // UW):
                    slot = wload(w_k[:, u * UW:(u + 1) * UW])
                    for m in range(MPU):
                        g = u * MPU + m
                        bank = next_bank()
                        for kc in range(KC):
                            mm(psum[bank][:], wr_t[:, slot, kc, m * P:(m + 1) * P], xn_t[:, kc], kc == 0, kc == KC - 1, reads=[('w', slot), ('xn', kc)], bank=bank)
                        S.op('act', 'activation', dict(out=kT_v[:, g, P:P + T], in_=psum[bank][:], func=AF.Identity, bias=C('b_k', g)), reads=[('ps', bank), 'consts'], writes=pg(32, 42))
                vslots = [wload(w_v[:, u * UW:(u + 1) * UW]) for u in range(512 // UW)]
                for tb in range(NTB):
                    bank = next_bank()
                    for u in range(512 // UW):
                        for kc in range(KC):
                            mm(psum[bank][:, u * UW:(u + 1) * UW], xn_t[:, kc, tb * P:(tb + 1) * P], wr_t[:, vslots[u], kc, :], kc == 0, kc == KC - 1, reads=[('w', vslots[u]), ('xn', kc)], bank=bank)
                    for dup in range(2):
                        S.op('dve', 'tensor_tensor', dict(out=vv_v[:, 1 + tb].rearrange('p (g u d) -> p g u d', u=2, d=64)[:, :, dup, :], in0=psum[bank][:].rearrange('p (g d) -> p g d', d=64), in1=bv_t[:].rearrange('p (g d) -> p g d', d=64), op=ALU.add), reads=[('ps', bank), 'bv'], writes=pg(42, 52))
                if not seq_first:
                    S.op('dve', 'tensor_copy', dict(out=kT_v[:, :, 0:P], in_=kprev_t[:]), reads=['kprev'], writes=pg(32, 42))
                    S.op('dve', 'tensor_copy', dict(out=vv_v[:, 0], in_=vprev_t[:]), reads=['vprev'], writes=pg(42, 52))
                r2 = r
                apply_norm(r2, 'b_norm', xn_t, lambda c: ('xn', c))
                for u in range(D // UW):
                    slot = wload(w_q[:, u * UW:(u + 1) * UW])
                    for m in range(MPU):
                        oc = u * MPU + m
                        bank = next_bank()
                        for kc in range(KC):
                            mm(psum[bank][:], wr_t[:, slot, kc, m * P:(m + 1) * P], xn_t[:, kc], kc == 0, kc == KC - 1, reads=[('w', slot), ('xn', kc)], bank=bank)
                        S.op('act', 'activation', dict(out=qT_v[:, oc], in_=psum[bank][:], func=AF.Identity, bias=C('b_q', oc)), reads=[('ps', bank), 'consts'], writes=[('big', oc)])
                def attn_scores(n, g):
                    kbs = [1] if seq_first and n == 0 else [0, 1]
                    c0 = kbs[0] * 2 * P
                    sb_ = [next_bank(), next_bank()]
                    pts = [nxt('pT', NPT), nxt('pT', NPT)]
                    for kb in kbs:
                        for a in range(2):
                            for par in range(2):
                                ch = 2 * g + a
                                rows = slice(par * 64, par * 64 + 64)
                                col = (kb * 2 + a) * P
                                mm(psum[sb_[par]][:, col:col + P], kT_v[rows, g, (n + kb) * P:(n + kb + 1) * P], qT_v[rows, ch, n * P:(n + 1) * P], True, True, reads=pg(32, 42) + [('big', ch)], bank=sb_[par])
                    nk = len(kbs)
                    for par in range(2):
                        pt = pts[par]
                        S.op('act', 'activation', dict(out=pT_t[:, pt, c0:T], in_=psum[sb_[par]][:, c0:T], func=AF.Exp, scale=0.125), reads=[('ps', sb_[par])], writes=[('pT', pt)])
                        S.op('dve', 'tensor_tensor', dict(out=pT_t[:, pt, c0:T].rearrange('p (k a q) -> p k a q', a=2, q=P), in0=pT_t[:, pt, c0:T].rearrange('p (k a q) -> p k a q', a=2, q=P), in1=masks_t[:, kbs[0] * P:2 * P].rearrange('p (k q) -> p k q', q=P).unsqueeze(2).broadcast_to([P, nk, 2, P]), op=ALU.mult), reads=[('pT', pt), 'masks'], writes=[('pT', pt)])
                    return kbs, pts

                def attn_pv(n, g, kbs, pts):
                    bo = next_bank()
                    bd = next_bank()
                    for par in range(2):
                        for i, kb in enumerate(kbs):
                            mm(psum[bo][:, par * 2 * P:(par + 1) * 2 * P], vv_v[:, n + kb, g * P:(g + 1) * P], pT_t[:, pts[par], kb * 2 * P:(kb + 1) * 2 * P], i == 0, i == len(kbs) - 1, reads=pg(42, 52) + [('pT', pts[par])], bank=bo)
                    for par in range(2):
                        for i, kb in enumerate(kbs):
                            mm(psum[bd][:, par * 2 * P:(par + 1) * 2 * P], ones1_t[:], pT_t[:, pts[par], kb * 2 * P:(kb + 1) * 2 * P], i == 0, i == len(kbs) - 1, reads=['ones1', ('pT', pts[par])], bank=bd)
                    s = nxt('sc', NSC)
                    S.op('dve', 'tensor_tensor', dict(out=sc_t[:, s].rearrange('p (b a q) -> p b a q', a=2, q=P), in0=psum[bd][:].rearrange('p (b a q) -> p b a q', a=2, q=P), in1=esink_t[:, 4 * g:4 * g + 4].rearrange('p (a b) -> p b a', b=2).unsqueeze(3).broadcast_to([P, 2, 2, P]), op=ALU.add), reads=[('ps', bd), 'esink'], writes=[('sc', s)])
                    S.op('act', 'activation', dict(out=sc_t[:, s], in_=sc_t[:, s], func=AF.Ln), reads=[('sc', s)], writes=[('sc', s)])
                    S.op('act', 'activation', dict(out=sc_t[:, s], in_=sc_t[:, s], func=AF.Exp, scale=-1.0), reads=[('sc', s)], writes=[('sc', s)])
                    for par in range(2):
                        rows = slice(par * 64, par * 64 + 64)
                        S.op('dve', 'tensor_tensor', dict(out=ao_v[rows, 2 * g:2 * g + 2, n * P:(n + 1) * P], in0=psum[bo][rows, par * 2 * P:(par + 1) * 2 * P].rearrange('p (a q) -> p a q', q=P), in1=sc_t[rows, s, par * 2 * P:(par + 1) * 2 * P].rearrange('p (a q) -> p a q', q=P), op=ALU.mult), reads=[('ps', bo), ('sc', s)], writes=[('big', 16 + 2 * g), ('big', 16 + 2 * g + 1)])

                ng_list = [(n, g) for n in range(NTB) for g in range(NG)]
                wo_slots = [wload(w_o[:, u * UW:(u + 1) * UW]) for u in range(NSLOT - 1)]
                SKEW = 2
                pend_pv = []
                for (n, g) in ng_list:
                    pend_pv.append((n, g) + attn_scores(n, g))
                    if len(pend_pv) > SKEW:
                        attn_pv(*pend_pv.pop(0))
                while pend_pv:
                    attn_pv(*pend_pv.pop(0))
                S.op('dve', 'tensor_copy', dict(out=kprev_t[:], in_=kT_v[:, :, NTB * P:(NTB + 1) * P]), reads=pg(32, 42), writes=['kprev'])
                S.op('dve', 'tensor_copy', dict(out=vprev_t[:], in_=vv_v[:, NTB]), reads=pg(42, 52), writes=['vprev'])
                for u in range(D // UW):
                    slot = wo_slots[u] if u < len(wo_slots) else wload(w_o[:, u * UW:(u + 1) * UW])
                    for m in range(MPU):
                        oc = u * MPU + m
                        bank = next_bank()
                        for kc in range(KC):
                            mm(psum[bank][:], wr_t[:, slot, kc, m * P:(m + 1) * P], ao_v[:, kc], kc == 0, kc == KC - 1, reads=[('w', slot), ('big', 16 + kc)], bank=bank)
                        S.op('dve', 'scalar_tensor_tensor', dict(out=h_t[:, oc], in0=psum[bank][:], scalar=C('b_o', oc), in1=h_t[:, oc], op0=ALU.add, op1=ALU.add), reads=[('ps', bank), ('h', oc), 'consts'], writes=[('h', oc)])
            if stage >= 4:
                mlp(1)
            if stage >= 5:
                r = rmsnorm('final_norm')
            for c in range(KC):
                s = nxt('sc', NSC)
                if stage >= 5:
                    S.op('dve', 'scalar_tensor_tensor', dict(out=sc_t[:, s], in0=h_t[:, c], scalar=C('final_norm', c), in1=st_t[:, r], op0=ALU.mult, op1=ALU.mult), reads=[('h', c), ('st', r), 'consts'], writes=[('sc', s)])
                else:
                    S.op('dve', 'tensor_copy', dict(out=sc_t[:, s], in_=h_t[:, c]), reads=[('h', c)], writes=[('sc', s)])
                bank = next_bank()
                for tb in range(NTB):
                    S.op('pe', 'transpose', dict(out=psum[bank][:, tb * P:(tb + 1) * P], in_=sc_t[:, s, tb * P:(tb + 1) * P], identity=ident_t[:]), reads=[('sc', s), 'ident'], writes=[('ps', bank)])
                S.op('act', 'activation', dict(out=ost_v[:, :, c * P:(c + 1) * P], in_=psum[bank][:].rearrange('p (b q) -> p b q', q=P), func=AF.Identity), reads=[('ps', bank)], writes=pg(32, 64))
            S.op('sp', 'dma_start', dict(out=out[tok0:tok0 + T, :].rearrange('(b p) d -> p b d', p=P), in_=ost_v), reads=pg(32, 64), dma='o')
        S.final_wait('sp', ['o'])
        dma_keys = sorted(set((o['dma'] for e in S.ENG for o in S.ops[e] if o['dma'] is not None)), key=str)
        sems_eng = {e: es.enter_context(nc.semaphore('s_' + e)) for e in S.ENG}
        sems_dma = {k: es.enter_context(nc.semaphore('d_%d' % i)) for i, k in enumerate(dma_keys)}
        block = es.enter_context(nc.Block())
        S.emit(nc, block, sems_eng, sems_dma)
    return nc

def _col(v):
    v = np.asarray(v, np.float32)
    return np.ascontiguousarray(v.reshape(-1, P).T)

def prep_shared(inp):
    f = lambda k: np.asarray(inp[k], np.float32)
    consts = np.zeros((P, NCONST), np.float32)

    def put(name, arr):
        consts[:, _c[name]:_c[name] + arr.shape[1]] = arr
    put('a_norm', _col(f('a_norm')[0]))
    b_in = f('a_b_in')[0]
    put('b_in_a', _col(b_in[:D]))
    put('b_in_g', _col(b_in[D:]))
    wdw = f('a_w_dw')[0]
    put('w_dw', np.concatenate([_col(wdw[j]) for j in range(CW)], axis=1))
    put('b_dw', _col(f('a_b_dw')[0]))
    put('ln_g', _col(f('a_ln_g')[0]))
    put('ln_b', _col(f('a_ln_b')[0]))
    put('b_out', _col(f('a_b_out')[0]))
    put('kv_norm', _col(f('kv_norm')))
    bk = f('b_k').reshape(NG, 1, 64)
    put('b_k', _col(np.repeat(bk, 2, axis=1).reshape(-1)))
    put('b_norm', _col(f('b_norm')[0]))
    put('b_q', _col(f('b_b_q')[0]))
    put('b_o', _col(f('b_b_o')[0]))
    put('mlp_norm0', _col(f('mlp_norm')[0]))
    put('mlp_norm1', _col(f('mlp_norm')[1]))
    put('final_norm', _col(f('final_norm')))
    put('eps_n', np.full((P, 1), NORM_EPS, np.float32))
    put('eps_ln', np.full((P, 1), LN_EPS, np.float32))
    w_in = f('a_w_in')[0]
    a_half = w_in[:, :D].reshape(D, KC, 1, P)
    g_half = w_in[:, D:].reshape(D, KC, 1, P)
    w_in_p = np.ascontiguousarray(np.concatenate([a_half, g_half], axis=2).reshape(D, 2 * D))
    wk = f('w_k').reshape(D, NG, 1, 64)
    wk_dup = np.ascontiguousarray(np.repeat(wk, 2, axis=2).reshape(D, 2 * 512))
    kk = np.arange(P)[:, None]
    qq = np.arange(P)[None, :]
    masks = np.concatenate([kk > qq, kk <= qq], axis=1).astype(np.float32)
    dwd = np.zeros((KC, KC, P, 2, P), np.float32)
    pp = np.arange(P)
    for j in range(CW):
        k, half = divmod(j, 2)
        dwd[:, k, pp, half, pp] = wdw[j].reshape(KC, P)
    dwd = dwd.reshape(KC, D, UW)
    return {'consts': consts, 'dwdiag': dwd, 'bv_b': np.ascontiguousarray(np.broadcast_to(f('b_v')[None, :], (P, 512))), 'sinks_b': np.ascontiguousarray(np.broadcast_to(f('b_sinks')[0][None, :], (P, 32))), 'masks': masks, 'ident': np.eye(P, dtype=np.float32), 'w_in': w_in_p, 'w_out': np.ascontiguousarray(f('a_w_out')[0]), 'w_k': wk_dup, 'w_v': np.ascontiguousarray(f('w_v')), 'w_q': np.ascontiguousarray(f('b_w_q')[0]), 'w_o': np.ascontiguousarray(f('b_w_o')[0]), 'w_up': np.ascontiguousarray(f('mlp_w_up')), 'w_down': np.ascontiguousarray(f('mlp_w_down'))}

def kernel(**inputs):
    n_cores = 8
    x = np.asarray(inputs['x'], np.float32)
    B = x.shape[0]
    shared = prep_shared(inputs)
    nc = build_nc()
    in_maps = []
    for c in range(n_cores):
        m = dict(shared)
        m['x'] = np.ascontiguousarray(x[c * NSEQ:(c + 1) * NSEQ].reshape(NSEQ * SEQ, D))
        in_maps.append(m)
    res = run_bass_kernel_spmd(nc, in_maps, core_ids=list(range(n_cores)))
    outs = [np.asarray(r['out'], np.float32).reshape(NSEQ, SEQ, D) for r in res.results]
    return np.concatenate(outs, axis=0).reshape(B, SEQ, D)
```

```python
import numpy as np
import concourse.bass as bass
import concourse.mybir as mybir
from concourse.bass_utils import run_bass_kernel_spmd
F32 = mybir.dt.float32
BF16 = mybir.dt.bfloat16
AF = mybir.ActivationFunctionType
ALU = mybir.AluOpType
P = 128
D = 2048
KC = D // P
T = 512
NTB = T // P
SEQ = 2048
NSEQ = 2
TPS = SEQ // T
DFF = 8192
HC = DFF // P
CW = 31
HALO = CW - 1
NG = 8
UW = 256
NSLOT = 4
NORM_EPS = 1e-06
LN_EPS = 1e-05
ND_DVE = 10
_c = {}
_off = 0
for _name, _n in [('a_norm', 16), ('b_in_a', 16), ('b_in_g', 16), ('w_dw', CW * 16), ('b_dw', 16), ('ln_g', 16), ('ln_b', 16), ('b_out', 16), ('kv_norm', 16), ('b_k', 8), ('b_norm', 16), ('b_q', 16), ('b_o', 16), ('mlp_norm0', 16), ('mlp_norm1', 16), ('final_norm', 16), ('eps_n', 1), ('eps_ln', 1)]:
    _c[_name] = _off
    _off += _n
NCONST = _off

class Sched:
    ENG = ('pe', 'act', 'dve', 'pool', 'sp')

    def __init__(self):
        self.ops = {e: [] for e in self.ENG}
        self.last_w = {}
        self.readers = {}
        self.dma_cnt = {}

    def op(self, eng, name, kw, reads=(), writes=(), dma=None):
        fn = (name, kw)
        ops = self.ops[eng]
        idx = len(ops)
        deps = set()

        def add(dep, kind):
            if dep is None:
                return
            if dep[0] == 'eng' and dep[1] == eng and (dma is None):
                if kind != 'raw' or eng == 'pe':
                    return
            deps.add(dep)
        for r in reads:
            add(self.last_w.get(r), 'raw')
        for w in writes:
            add(self.last_w.get(w), 'waw')
            for d in self.readers.get(w, {}).values():
                add(d, 'war')
        if dma is None:
            me = ('eng', eng, idx)
        else:
            self.dma_cnt[dma] = self.dma_cnt.get(dma, 0) + 1
            me = ('dma', dma, self.dma_cnt[dma])
        for r in reads:
            self.readers.setdefault(r, {})[eng, dma] = me
        for w in writes:
            self.last_w[w] = me
            self.readers[w] = {}
        ops.append({'fn': fn, 'deps': deps, 'dma': dma, 'signal': False})
        return me

    def final_wait(self, eng, keys):
        deps = set()
        for k in keys:
            deps.add(('dma', k, self.dma_cnt[k]))
        self.ops[eng].append({'fn': None, 'deps': deps, 'dma': None, 'signal': False})

    def emit(self, nc, block, sems_eng, sems_dma):
        for e in self.ENG:
            for o in self.ops[e]:
                for d in o['deps']:
                    if d[0] == 'eng':
                        self.ops[d[1]][d[2]]['signal'] = True
        signo = {}
        for e in self.ENG:
            n = 0
            for i, o in enumerate(self.ops[e]):
                if o['signal']:
                    n += 1
                    signo[e, i] = n
        self.nsig = {e: sum((1 for o in self.ops[e] if o['signal'])) for e in self.ENG}

        def run(e, engine):
            waited = {}
            for i, o in enumerate(self.ops[e]):
                need = {}
                for d in o['deps']:
                    if d[0] == 'eng':
                        key = ('eng', d[1])
                        val = signo[d[1], d[2]]
                    else:
                        key = ('dma', d[1])
                        val = 16 * d[2]
                    if waited.get(key, 0) >= val:
                        continue
                    need[key] = max(need.get(key, 0), val)
                for key, val in need.items():
                    sem = sems_eng[key[1]] if key[0] == 'eng' else sems_dma[key[1]]
                    engine.wait_ge(sem, val)
                    waited[key] = val
                if o['fn'] is None:
                    continue
                ins = getattr(engine, o['fn'][0])(**o['fn'][1])
                if o['dma'] is not None:
                    ins.then_inc(sems_dma[o['dma']], 16)
                elif o['signal']:
                    ins.then_inc(sems_eng[e], 1)
        block.tensor(lambda eng: run('pe', eng))
        block.scalar(lambda eng: run('act', eng))
        block.vector(lambda eng: run('dve', eng))
        block.gpsimd(lambda eng: run('pool', eng))
        block.sync(lambda eng: run('sp', eng))

def build_nc(n_tiles=NSEQ * TPS, stage=99):
    nc = bass.Bass('TRN2', target_bir_lowering=False)
    NTOK = NSEQ * SEQ
    dram = {}

    def din(name, shape):
        dram[name] = nc.dram_tensor(name, list(shape), F32, kind='ExternalInput').ap()
        return dram[name]
    x = din('x', [NTOK, D])
    consts_d = din('consts', [P, NCONST])
    bv_d = din('bv_b', [P, 512])
    sinks_d = din('sinks_b', [P, 32])
    masks_d = din('masks', [P, 2 * P])
    ident_d = din('ident', [P, P])
    w_in = din('w_in', [D, 2 * D])
    dwdiag = din('dwdiag', [KC, D, UW])
    w_out = din('w_out', [D, D])
    w_k = din('w_k', [D, 2 * 512])
    w_v = din('w_v', [D, 512])
    w_q = din('w_q', [D, D])
    w_o = din('w_o', [D, D])
    w_up = din('w_up', [2, D, DFF])
    w_down = din('w_down', [2, DFF, D])
    out = nc.dram_tensor('out', [NTOK, D], F32, kind='ExternalOutput').ap()
    S = Sched()
    from contextlib import ExitStack
    with ExitStack() as es:

        def sb(name, shape, dt):
            return es.enter_context(nc.sbuf_tensor('sb_' + name, list(shape), dt))
        h_t = sb('h', [P, KC, T], F32)
        xn_t = sb('xn', [P, KC, T], BF16)
        big_t = sb('big', [P, HC * T], BF16)
        wr_t = sb('wring', [P, NSLOT, KC, UW], BF16)
        consts_t = sb('consts', [P, NCONST], F32)
        bv_t = sb('bv', [P, 512], F32)
        sinks_t = sb('sinks', [P, 32], F32)
        esink_t = sb('esink', [P, 32], F32)
        masks_t = sb('masks', [P, 2 * P], BF16)
        ident_t = sb('ident', [P, P], F32)
        onesD_t = sb('onesD', [P, P], BF16)
        ones1_t = sb('ones1', [P, P], BF16)
        NSQ = 3
        sq_t = sb('sq', [P, NSQ, T], BF16)
        NSC = 5
        sc_t = sb('sc', [P, NSC, T], F32)
        NST = 4
        st_t = sb('st', [P, NST, T], F32)
        NUB = 3
        NYP = 2
        yp_t = sb('yp', [P, NYP, T], F32)
        ub_t = sb('ub', [P, NUB, HALO + T], BF16)
        halo_t = sb('halo', [P, KC, HALO], BF16)
        kprev_t = sb('kprev', [P, NG, P], BF16)
        vprev_t = sb('vprev', [P, NG * P], BF16)
        NPT = 8
        pT_t = sb('pT', [P, NPT, T], BF16)
        psum = [es.enter_context(nc.psum_tensor('ps%d' % i, [P, T], F32)) for i in range(8)]
        big_bf = big_t.ap()
        big_f = big_t.bitcast(F32).ap()
        hid_v = big_bf.rearrange('p (c t) -> p c t', t=T)
        y_v = big_f[:, 0:KC * T].rearrange('p (c t) -> p c t', t=T)
        xst_v = big_f[:, 0:NTB * D].rearrange('p (b d) -> p b d', d=D)
        ost_v = big_f[:, NTB * D:2 * NTB * D].rearrange('p (b d) -> p b d', d=D)
        qT_v = big_bf[:, 0:KC * T].rearrange('p (c t) -> p c t', t=T)
        ao_v = big_bf[:, KC * T:2 * KC * T].rearrange('p (c t) -> p c t', t=T)
        KTW = (NTB + 1) * P
        kT_v = big_bf[:, 32 * T:32 * T + NG * KTW].rearrange('p (g k) -> p g k', k=KTW)
        vv_v = big_bf[:, 42 * T:42 * T + (NTB + 1) * NG * P].rearrange('p (b n) -> p b n', n=NG * P)

        def pg(lo, hi):
            return [('big', k) for k in range(lo, hi)]

        def C(name, col=0, rows=slice(0, P)):
            o = _c[name] + col
            return consts_t[rows, o:o + 1]
        bank_ctr = [0]

        reserved = set()

        def next_bank():
            while True:
                b = bank_ctr[0] % 8
                bank_ctr[0] += 1
                if b not in reserved:
                    return b
        rot = {'sq': 0, 'sc': 0, 'st': 0, 'ub': 0, 'pT': 0, 'yp': 0}

        def nxt(name, n):
            v = rot[name] % n
            rot[name] += 1
            return v
        wctr = [0]
        pending_units = []

        NUNITS = 190
        wcache = nc.dram_tensor('wcache', [NUNITS, P, KC * UW], BF16).ap()
        ucnt = [0]
        cur_tile = [0]

        def wload(src_ap):
            slot = wctr[0] % NSLOT
            wctr[0] += 1
            uid = ucnt[0]
            ucnt[0] += 1
            assert uid < NUNITS
            cview = wcache[uid].rearrange('p (k n) -> p k n', n=UW)
            fct = uid % 2 if n_tiles > 1 else 0
            if cur_tile[0] <= fct:
                S.op('pool', 'dma_start', dict(out=wr_t[:, slot], in_=src_ap.rearrange('(k p) n -> p k n', p=P)), writes=[('w', slot)], dma=('w', slot))
                if cur_tile[0] == fct:
                    S.op('sp', 'dma_start', dict(out=cview, in_=wr_t[:, slot]), reads=[('w', slot)], writes=[('wc', uid)], dma=('wb', slot))
            else:
                S.op('sp', 'dma_start', dict(out=wr_t[:, slot], in_=cview), reads=[('wc', uid)], writes=[('w', slot)], dma=('wh', slot))
            return slot

        def mm(out_ap, lhsT, rhs, start, stop, reads, bank):
            S.op('pe', 'matmul', dict(out=out_ap, lhsT=lhsT, rhs=rhs, start=start, stop=stop), reads=reads, writes=[('ps', bank)])
        S.op('sp', 'dma_start', dict(out=consts_t[:], in_=consts_d), writes=['consts'], dma='c_consts')
        S.op('sp', 'dma_start', dict(out=bv_t[:], in_=bv_d), writes=['bv'], dma='c_bv')
        S.op('sp', 'dma_start', dict(out=sinks_t[:], in_=sinks_d), writes=['sinks'], dma='c_sinks')
        S.op('sp', 'dma_start', dict(out=ident_t[:], in_=ident_d), writes=['ident'], dma='c_ident')
        S.op('pool', 'dma_start', dict(out=masks_t[:], in_=masks_d), writes=['masks'], dma='c_masks')
        S.op('dve', 'memset', dict(ap=onesD_t[:], constant=1.0 / D), writes=['onesD'])
        S.op('dve', 'memset', dict(ap=ones1_t[:], constant=1.0), writes=['ones1'])
        S.op('act', 'activation', dict(out=esink_t[:], in_=sinks_t[:], func=AF.Exp), reads=['sinks'], writes=['esink'])

        def rmsnorm(gname, dst_reads_extra=()):
            bank = next_bank()
            for c in range(KC):
                q = nxt('sq', NSQ)
                S.op('act', 'activation', dict(out=sq_t[:, q], in_=h_t[:, c], func=AF.Square), reads=[('h', c)], writes=[('sq', q)])
                mm(psum[bank][:], onesD_t[:], sq_t[:, q], c == 0, c == KC - 1, reads=[('sq', q), 'onesD'], bank=bank)
            r = nxt('st', NST)
            S.op('act', 'activation', dict(out=st_t[:, r], in_=psum[bank][:], func=AF.Ln, bias=C('eps_n')), reads=[('ps', bank), 'consts'], writes=[('st', r)])
            S.op('act', 'activation', dict(out=st_t[:, r], in_=st_t[:, r], func=AF.Exp, scale=-0.5), reads=[('st', r)], writes=[('st', r)])
            return r

        def apply_norm(r, gname, dst, dst_res):
            for c in range(KC):
                S.op('dve', 'scalar_tensor_tensor', dict(out=dst[:, c], in0=h_t[:, c], scalar=C(gname, c), in1=st_t[:, r], op0=ALU.mult, op1=ALU.mult), reads=[('h', c), ('st', r), 'consts'], writes=[dst_res(c)])

        def proj_units(wsrc_fn, n_units):
            for u in range(n_units):
                yield (u, wload(wsrc_fn(u)))

        def mlp(layer):
            r = rmsnorm('mlp_norm%d' % layer)
            apply_norm(r, 'mlp_norm%d' % layer, xn_t, lambda c: ('xn', c))
            for u in range(DFF // UW):
                slot = wload(w_up[layer][:, u * UW:(u + 1) * UW])
                for m in range(UW // P):
                    hc = u * (UW // P) + m
                    bank = next_bank()
                    for kc in range(KC):
                        mm(psum[bank][:], wr_t[:, slot, kc, m * P:(m + 1) * P], xn_t[:, kc], kc == 0, kc == KC - 1, reads=[('w', slot), ('xn', kc)], bank=bank)
                    s = nxt('sc', NSC)
                    S.op('act', 'activation', dict(out=sc_t[:, s], in_=psum[bank][:], func=AF.Relu), reads=[('ps', bank)], writes=[('sc', s)])
                    S.op('dve', 'tensor_tensor', dict(out=hid_v[:, hc], in0=sc_t[:, s], in1=sc_t[:, s], op=ALU.mult), reads=[('sc', s)], writes=[('big', hc)])
            MPU = UW // P
            for ng in range(D // UW):
                banks = [next_bank() for _ in range(MPU)]
                for kg in range(DFF // D):
                    slot = wload(w_down[layer][kg * D:(kg + 1) * D, ng * UW:(ng + 1) * UW])
                    for m in range(MPU):
                        for kc in range(KC):
                            hc = kg * KC + kc
                            mm(psum[banks[m]][:], wr_t[:, slot, kc, m * P:(m + 1) * P], hid_v[:, hc], kg == 0 and kc == 0, kg == DFF // D - 1 and kc == KC - 1, reads=[('w', slot), ('big', hc)], bank=banks[m])
                for m in range(MPU):
                    oc = ng * MPU + m
                    S.op('dve', 'tensor_tensor', dict(out=h_t[:, oc], in0=psum[banks[m]][:], in1=h_t[:, oc], op=ALU.add), reads=[('ps', banks[m]), ('h', oc)], writes=[('h', oc)])
        for it in range(n_tiles):
            seq_first = it % TPS == 0
            cur_tile[0] = it
            ucnt[0] = 0
            tok0 = it * T
            S.op('sp', 'dma_start', dict(out=xst_v, in_=x[tok0:tok0 + T, :].rearrange('(b p) d -> p b d', p=P)), writes=pg(0, 32), dma='x')
            for c in range(KC):
                bank = next_bank()
                for tb in range(NTB):
                    S.op('pe', 'transpose', dict(out=psum[bank][:, tb * P:(tb + 1) * P], in_=xst_v[:, tb, c * P:(c + 1) * P], identity=ident_t[:]), reads=pg(8 * tb, 8 * tb + 8) + ['ident'], writes=[('ps', bank)])
                S.op('dve', 'tensor_copy', dict(out=h_t[:, c], in_=psum[bank][:]), reads=[('ps', bank)], writes=[('h', c)])
            if stage >= 1:
                r = rmsnorm('a_norm')
                apply_norm(r, 'a_norm', xn_t, lambda c: ('xn', c))
                if seq_first:
                    S.op('dve', 'memset', dict(ap=halo_t[:], constant=0.0), writes=[('halo', c) for c in range(KC)])
                bm = next_bank()
                bq = next_bank()
                reserved.update([bm, bq])
                pend = []

                def ln_stats(c, first, last):
                    q1 = nxt('sq', NSQ)
                    S.op('act', 'activation', dict(out=sq_t[:, q1], in_=y_v[:, c], func=AF.Identity), reads=pg(2 * c, 2 * c + 2), writes=[('sq', q1)])
                    mm(psum[bm][:], onesD_t[:], sq_t[:, q1], first, last, reads=[('sq', q1), 'onesD'], bank=bm)
                    q2 = nxt('sq', NSQ)
                    S.op('act', 'activation', dict(out=sq_t[:, q2], in_=y_v[:, c], func=AF.Square), reads=pg(2 * c, 2 * c + 2), writes=[('sq', q2)])
                    mm(psum[bq][:], onesD_t[:], sq_t[:, q2], first, last, reads=[('sq', q2), 'onesD'], bank=bq)

                for c in range(KC):
                    slot = wload(w_in[:, c * UW:(c + 1) * UW])
                    slot_d = wload(dwdiag[c])
                    ba = next_bank()
                    bg = next_bank()
                    for kc in range(KC):
                        mm(psum[ba][:], wr_t[:, slot, kc, 0:P], xn_t[:, kc], kc == 0, kc == KC - 1, reads=[('w', slot), ('xn', kc)], bank=ba)
                    for kc in range(KC):
                        mm(psum[bg][:], wr_t[:, slot, kc, P:2 * P], xn_t[:, kc], kc == 0, kc == KC - 1, reads=[('w', slot), ('xn', kc)], bank=bg)
                    s = nxt('sc', NSC)
                    S.op('act', 'activation', dict(out=sc_t[:, s], in_=psum[bg][:], func=AF.Sigmoid, bias=C('b_in_g', c)), reads=[('ps', bg), 'consts'], writes=[('sc', s)])
                    ub = nxt('ub', NUB)
                    S.op('dve', 'tensor_copy', dict(out=ub_t[:, ub, 0:HALO], in_=halo_t[:, c]), reads=[('halo', c)], writes=[('ub', ub)])
                    S.op('dve', 'scalar_tensor_tensor', dict(out=ub_t[:, ub, HALO:HALO + T], in0=psum[ba][:], scalar=C('b_in_a', c), in1=sc_t[:, s], op0=ALU.add, op1=ALU.mult), reads=[('ps', ba), ('sc', s), 'consts'], writes=[('ub', ub)])
                    S.op('dve', 'tensor_copy', dict(out=halo_t[:, c], in_=ub_t[:, ub, T:T + HALO]), reads=[('ub', ub)], writes=[('halo', c)])
                    if len(pend) > 1:
                        pc = pend.pop(0)
                        ln_stats(pc, pc == 0, False)
                    bc = next_bank()
                    NPE = CW - ND_DVE
                    for j in range(NPE):
                        mm(psum[bc][:], wr_t[:, slot_d, j // 2, (j % 2) * P:(j % 2 + 1) * P], ub_t[:, ub, j:j + T], j == 0, j == NPE - 1, reads=[('w', slot_d), ('ub', ub)], bank=bc)
                    yp = nxt('yp', NYP)
                    for j in range(NPE, CW):
                        if j == NPE:
                            S.op('dve', 'tensor_scalar', dict(out=yp_t[:, yp], in0=ub_t[:, ub, j:j + T], scalar1=C('w_dw', j * 16 + c), scalar2=None, op0=ALU.mult), reads=[('ub', ub), 'consts'], writes=[('yp', yp)])
                        else:
                            S.op('dve', 'scalar_tensor_tensor', dict(out=yp_t[:, yp], in0=ub_t[:, ub, j:j + T], scalar=C('w_dw', j * 16 + c), in1=yp_t[:, yp], op0=ALU.mult, op1=ALU.add), reads=[('ub', ub), ('yp', yp), 'consts'], writes=[('yp', yp)])
                    S.op('dve', 'scalar_tensor_tensor', dict(out=y_v[:, c], in0=psum[bc][:], scalar=C('b_dw', c), in1=yp_t[:, yp], op0=ALU.add, op1=ALU.add), reads=[('ps', bc), ('yp', yp), 'consts'], writes=pg(2 * c, 2 * c + 2))
                    pend.append(c)
                while pend:
                    pc = pend.pop(0)
                    ln_stats(pc, pc == 0, pc == KC - 1)
                reserved.clear()
                rm = nxt('st', NST)
                rv = nxt('st', NST)
                rn = nxt('st', NST)
                S.op('dve', 'tensor_copy', dict(out=st_t[:, rm], in_=psum[bm][:]), reads=[('ps', bm)], writes=[('st', rm)])
                S.op('dve', 'tensor_tensor', dict(out=st_t[:, rn], in0=st_t[:, rm], in1=st_t[:, rm], op=ALU.mult), reads=[('st', rm)], writes=[('st', rn)])
                S.op('dve', 'tensor_tensor', dict(out=st_t[:, rv], in0=psum[bq][:], in1=st_t[:, rn], op=ALU.subtract), reads=[('ps', bq), ('st', rn)], writes=[('st', rv)])
                S.op('act', 'activation', dict(out=st_t[:, rv], in_=st_t[:, rv], func=AF.Ln, bias=C('eps_ln')), reads=[('st', rv), 'consts'], writes=[('st', rv)])
                S.op('act', 'activation', dict(out=st_t[:, rv], in_=st_t[:, rv], func=AF.Exp, scale=-0.5), reads=[('st', rv)], writes=[('st', rv)])
                S.op('dve', 'scalar_tensor_tensor', dict(out=st_t[:, rn], in0=st_t[:, rm], scalar=-1.0, in1=st_t[:, rv], op0=ALU.mult, op1=ALU.mult), reads=[('st', rm), ('st', rv)], writes=[('st', rn)])
                for c in range(KC):
                    s = nxt('sc', NSC)
                    S.op('dve', 'tensor_tensor', dict(out=sc_t[:, s], in0=y_v[:, c], in1=st_t[:, rv], op=ALU.mult), reads=pg(2 * c, 2 * c + 2) + [('st', rv)], writes=[('sc', s)])
                    S.op('dve', 'tensor_tensor', dict(out=sc_t[:, s], in0=sc_t[:, s], in1=st_t[:, rn], op=ALU.add), reads=[('sc', s), ('st', rn)], writes=[('sc', s)])
                    S.op('act', 'activation', dict(out=xn_t[:, c], in_=sc_t[:, s], func=AF.Silu, bias=C('ln_b', c), scale=C('ln_g', c)), reads=[('sc', s), 'consts'], writes=[('xn', c)])
                MPU = UW // P
                for u in range(D // UW):
                    slot = wload(w_out[:, u * UW:(u + 1) * UW])
                    for m in range(MPU):
                        oc = u * MPU + m
                        bank = next_bank()
                        for kc in range(KC):
                            mm(psum[bank][:], wr_t[:, slot, kc, m * P:(m + 1) * P], xn_t[:, kc], kc == 0, kc == KC - 1, reads=[('w', slot), ('xn', kc)], bank=bank)
                        S.op('dve', 'scalar_tensor_tensor', dict(out=h_t[:, oc], in0=psum[bank][:], scalar=C('b_out', oc), in1=h_t[:, oc], op0=ALU.add, op1=ALU.add), reads=[('ps', bank), ('h', oc), 'consts'], writes=[('h', oc)])
            if stage >= 2:
                mlp(0)
            if stage >= 3:
                MPU = UW // P
                r = rmsnorm('kv_norm')
                apply_norm(r, 'kv_norm', xn_t, lambda c: ('xn', c))
                for u in range(2 * 512 // UW):
                    slot = wload(w_k[:, u * UW:(u + 1) * UW])
                    for m in range(MPU):
                        g = u * MPU + m
                        bank = next_bank()
                        for kc in range(KC):
                            mm(psum[bank][:], wr_t[:, slot, kc, m * P:(m + 1) * P], xn_t[:, kc], kc == 0, kc == KC - 1, reads=[('w', slot), ('xn', kc)], bank=bank)
                        S.op('act', 'activation', dict(out=kT_v[:, g, P:P + T], in_=psum[bank][:], func=AF.Identity, bias=C('b_k', g)), reads=[('ps', bank), 'consts'], writes=pg(32, 42))
                vslots = [wload(w_v[:, u * UW:(u + 1) * UW]) for u in range(512 // UW)]
                for tb in range(NTB):
                    bank = next_bank()
                    for u in range(512 // UW):
                        for kc in range(KC):
                            mm(psum[bank][:, u * UW:(u + 1) * UW], xn_t[:, kc, tb * P:(tb + 1) * P], wr_t[:, vslots[u], kc, :], kc == 0, kc == KC - 1, reads=[('w', vslots[u]), ('xn', kc)], bank=bank)
                    for dup in range(2):
                        S.op('dve', 'tensor_tensor', dict(out=vv_v[:, 1 + tb].rearrange('p (g u d) -> p g u d', u=2, d=64)[:, :, dup, :], in0=psum[bank][:].rearrange('p (g d) -> p g d', d=64), in1=bv_t[:].rearrange('p (g d) -> p g d', d=64), op=ALU.add), reads=[('ps', bank), 'bv'], writes=pg(42, 52))
                if not seq_first:
                    S.op('dve', 'tensor_copy', dict(out=kT_v[:, :, 0:P], in_=kprev_t[:]), reads=['kprev'], writes=pg(32, 42))
                    S.op('dve', 'tensor_copy', dict(out=vv_v[:, 0], in_=vprev_t[:]), reads=['vprev'], writes=pg(42, 52))
                r2 = r
                apply_norm(r2, 'b_norm', xn_t, lambda c: ('xn', c))
                for u in range(D // UW):
                    slot = wload(w_q[:, u * UW:(u + 1) * UW])
                    for m in range(MPU):
                        oc = u * MPU + m
                        bank = next_bank()
                        for kc in range(KC):
                            mm(psum[bank][:], wr_t[:, slot, kc, m * P:(m + 1) * P], xn_t[:, kc], kc == 0, kc == KC - 1, reads=[('w', slot), ('xn', kc)], bank=bank)
                        S.op('act', 'activation', dict(out=qT_v[:, oc], in_=psum[bank][:], func=AF.Identity, bias=C('b_q', oc)), reads=[('ps', bank), 'consts'], writes=[('big', oc)])
                def attn_scores(n, g):
                    kbs = [1] if seq_first and n == 0 else [0, 1]
                    c0 = kbs[0] * 2 * P
                    sb_ = [next_bank(), next_bank()]
                    pts = [nxt('pT', NPT), nxt('pT', NPT)]
                    for kb in kbs:
                        for a in range(2):
                            for par in range(2):
                                ch = 2 * g + a
                                rows = slice(par * 64, par * 64 + 64)
                                col = (kb * 2 + a) * P
                                mm(psum[sb_[par]][:, col:col + P], kT_v[rows, g, (n + kb) * P:(n + kb + 1) * P], qT_v[rows, ch, n * P:(n + 1) * P], True, True, reads=pg(32, 42) + [('big', ch)], bank=sb_[par])
                    nk = len(kbs)
                    for par in range(2):
                        pt = pts[par]
                        S.op('act', 'activation', dict(out=pT_t[:, pt, c0:T], in_=psum[sb_[par]][:, c0:T], func=AF.Exp, scale=0.125), reads=[('ps', sb_[par])], writes=[('pT', pt)])
                        S.op('dve', 'tensor_tensor', dict(out=pT_t[:, pt, c0:T].rearrange('p (k a q) -> p k a q', a=2, q=P), in0=pT_t[:, pt, c0:T].rearrange('p (k a q) -> p k a q', a=2, q=P), in1=masks_t[:, kbs[0] * P:2 * P].rearrange('p (k q) -> p k q', q=P).unsqueeze(2).broadcast_to([P, nk, 2, P]), op=ALU.mult), reads=[('pT', pt), 'masks'], writes=[('pT', pt)])
                    return kbs, pts

                def attn_pv(n, g, kbs, pts):
                    bo = next_bank()
                    bd = next_bank()
                    for par in range(2):
                        for i, kb in enumerate(kbs):
                            mm(psum[bo][:, par * 2 * P:(par + 1) * 2 * P], vv_v[:, n + kb, g * P:(g + 1) * P], pT_t[:, pts[par], kb * 2 * P:(kb + 1) * 2 * P], i == 0, i == len(kbs) - 1, reads=pg(42, 52) + [('pT', pts[par])], bank=bo)
                    for par in range(2):
                        for i, kb in enumerate(kbs):
                            mm(psum[bd][:, par * 2 * P:(par + 1) * 2 * P], ones1_t[:], pT_t[:, pts[par], kb * 2 * P:(kb + 1) * 2 * P], i == 0, i == len(kbs) - 1, reads=['ones1', ('pT', pts[par])], bank=bd)
                    s = nxt('sc', NSC)
                    S.op('dve', 'tensor_tensor', dict(out=sc_t[:, s].rearrange('p (b a q) -> p b a q', a=2, q=P), in0=psum[bd][:].rearrange('p (b a q) -> p b a q', a=2, q=P), in1=esink_t[:, 4 * g:4 * g + 4].rearrange('p (a b) -> p b a', b=2).unsqueeze(3).broadcast_to([P, 2, 2, P]), op=ALU.add), reads=[('ps', bd), 'esink'], writes=[('sc', s)])
                    S.op('act', 'activation', dict(out=sc_t[:, s], in_=sc_t[:, s], func=AF.Ln), reads=[('sc', s)], writes=[('sc', s)])
                    S.op('act', 'activation', dict(out=sc_t[:, s], in_=sc_t[:, s], func=AF.Exp, scale=-1.0), reads=[('sc', s)], writes=[('sc', s)])
                    for par in range(2):
                        rows = slice(par * 64, par * 64 + 64)
                        S.op('dve', 'tensor_tensor', dict(out=ao_v[rows, 2 * g:2 * g + 2, n * P:(n + 1) * P], in0=psum[bo][rows, par * 2 * P:(par + 1) * 2 * P].rearrange('p (a q) -> p a q', q=P), in1=sc_t[rows, s, par * 2 * P:(par + 1) * 2 * P].rearrange('p (a q) -> p a q', q=P), op=ALU.mult), reads=[('ps', bo), ('sc', s)], writes=[('big', 16 + 2 * g), ('big', 16 + 2 * g + 1)])

                ng_list = [(n, g) for n in range(NTB) for g in range(NG)]
                wo_slots = [wload(w_o[:, u * UW:(u + 1) * UW]) for u in range(NSLOT - 1)]
                SKEW = 2
                pend_pv = []
                for (n, g) in ng_list:
                    pend_pv.append((n, g) + attn_scores(n, g))
                    if len(pend_pv) > SKEW:
                        attn_pv(*pend_pv.pop(0))
                while pend_pv:
                    attn_pv(*pend_pv.pop(0))
                S.op('dve', 'tensor_copy', dict(out=kprev_t[:], in_=kT_v[:, :, NTB * P:(NTB + 1) * P]), reads=pg(32, 42), writes=['kprev'])
                S.op('dve', 'tensor_copy', dict(out=vprev_t[:], in_=vv_v[:, NTB]), reads=pg(42, 52), writes=['vprev'])
                for u in range(D // UW):
                    slot = wo_slots[u] if u < len(wo_slots) else wload(w_o[:, u * UW:(u + 1) * UW])
                    for m in range(MPU):
                        oc = u * MPU + m
                        bank = next_bank()
                        for kc in range(KC):
                            mm(psum[bank][:], wr_t[:, slot, kc, m * P:(m + 1) * P], ao_v[:, kc], kc == 0, kc == KC - 1, reads=[('w', slot), ('big', 16 + kc)], bank=bank)
                        S.op('dve', 'scalar_tensor_tensor', dict(out=h_t[:, oc], in0=psum[bank][:], scalar=C('b_o', oc), in1=h_t[:, oc], op0=ALU.add, op1=ALU.add), reads=[('ps', bank), ('h', oc), 'consts'], writes=[('h', oc)])
            if stage >= 4:
                mlp(1)
            if stage >= 5:
                r = rmsnorm('final_norm')
            for c in range(KC):
                s = nxt('sc', NSC)
                if stage >= 5:
                    S.op('dve', 'scalar_tensor_tensor', dict(out=sc_t[:, s], in0=h_t[:, c], scalar=C('final_norm', c), in1=st_t[:, r], op0=ALU.mult, op1=ALU.mult), reads=[('h', c), ('st', r), 'consts'], writes=[('sc', s)])
                else:
                    S.op('dve', 'tensor_copy', dict(out=sc_t[:, s], in_=h_t[:, c]), reads=[('h', c)], writes=[('sc', s)])
                bank = next_bank()
                for tb in range(NTB):
                    S.op('pe', 'transpose', dict(out=psum[bank][:, tb * P:(tb + 1) * P], in_=sc_t[:, s, tb * P:(tb + 1) * P], identity=ident_t[:]), reads=[('sc', s), 'ident'], writes=[('ps', bank)])
                S.op('act', 'activation', dict(out=ost_v[:, :, c * P:(c + 1) * P], in_=psum[bank][:].rearrange('p (b q) -> p b q', q=P), func=AF.Identity), reads=[('ps', bank)], writes=pg(32, 64))
            S.op('sp', 'dma_start', dict(out=out[tok0:tok0 + T, :].rearrange('(b p) d -> p b d', p=P), in_=ost_v), reads=pg(32, 64), dma='o')
        S.final_wait('sp', ['o'])
        dma_keys = sorted(set((o['dma'] for e in S.ENG for o in S.ops[e] if o['dma'] is not None)), key=str)
        sems_eng = {e: es.enter_context(nc.semaphore('s_' + e)) for e in S.ENG}
        sems_dma = {k: es.enter_context(nc.semaphore('d_%d' % i)) for i, k in enumerate(dma_keys)}
        block = es.enter_context(nc.Block())
        S.emit(nc, block, sems_eng, sems_dma)
    return nc

def _col(v):
    v = np.asarray(v, np.float32)
    return np.ascontiguousarray(v.reshape(-1, P).T)

def prep_shared(inp):
    f = lambda k: np.asarray(inp[k], np.float32)
    consts = np.zeros((P, NCONST), np.float32)

    def put(name, arr):
        consts[:, _c[name]:_c[name] + arr.shape[1]] = arr
    put('a_norm', _col(f('a_norm')[0]))
    b_in = f('a_b_in')[0]
    put('b_in_a', _col(b_in[:D]))
    put('b_in_g', _col(b_in[D:]))
    wdw = f('a_w_dw')[0]
    put('w_dw', np.concatenate([_col(wdw[j]) for j in range(CW)], axis=1))
    put('b_dw', _col(f('a_b_dw')[0]))
    put('ln_g', _col(f('a_ln_g')[0]))
    put('ln_b', _col(f('a_ln_b')[0]))
    put('b_out', _col(f('a_b_out')[0]))
    put('kv_norm', _col(f('kv_norm')))
    bk = f('b_k').reshape(NG, 1, 64)
    put('b_k', _col(np.repeat(bk, 2, axis=1).reshape(-1)))
    put('b_norm', _col(f('b_norm')[0]))
    put('b_q', _col(f('b_b_q')[0]))
    put('b_o', _col(f('b_b_o')[0]))
    put('mlp_norm0', _col(f('mlp_norm')[0]))
    put('mlp_norm1', _col(f('mlp_norm')[1]))
    put('final_norm', _col(f('final_norm')))
    put('eps_n', np.full((P, 1), NORM_EPS, np.float32))
    put('eps_ln', np.full((P, 1), LN_EPS, np.float32))
    w_in = f('a_w_in')[0]
    a_half = w_in[:, :D].reshape(D, KC, 1, P)
    g_half = w_in[:, D:].reshape(D, KC, 1, P)
    w_in_p = np.ascontiguousarray(np.concatenate([a_half, g_half], axis=2).reshape(D, 2 * D))
    wk = f('w_k').reshape(D, NG, 1, 64)
    wk_dup = np.ascontiguousarray(np.repeat(wk, 2, axis=2).reshape(D, 2 * 512))
    kk = np.arange(P)[:, None]
    qq = np.arange(P)[None, :]
    masks = np.concatenate([kk > qq, kk <= qq], axis=1).astype(np.float32)
    dwd = np.zeros((KC, KC, P, 2, P), np.float32)
    pp = np.arange(P)
    for j in range(CW):
        k, half = divmod(j, 2)
        dwd[:, k, pp, half, pp] = wdw[j].reshape(KC, P)
    dwd = dwd.reshape(KC, D, UW)
    return {'consts': consts, 'dwdiag': dwd, 'bv_b': np.ascontiguousarray(np.broadcast_to(f('b_v')[None, :], (P, 512))), 'sinks_b': np.ascontiguousarray(np.broadcast_to(f('b_sinks')[0][None, :], (P, 32))), 'masks': masks, 'ident': np.eye(P, dtype=np.float32), 'w_in': w_in_p, 'w_out': np.ascontiguousarray(f('a_w_out')[0]), 'w_k': wk_dup, 'w_v': np.ascontiguousarray(f('w_v')), 'w_q': np.ascontiguousarray(f('b_w_q')[0]), 'w_o': np.ascontiguousarray(f('b_w_o')[0]), 'w_up': np.ascontiguousarray(f('mlp_w_up')), 'w_down': np.ascontiguousarray(f('mlp_w_down'))}

def kernel(**inputs):
    n_cores = 8
    x = np.asarray(inputs['x'], np.float32)
    B = x.shape[0]
    shared = prep_shared(inputs)
    nc = build_nc()
    in_maps = []
    for c in range(n_cores):
        m = dict(shared)
        m['x'] = np.ascontiguousarray(x[c * NSEQ:(c + 1) * NSEQ].reshape(NSEQ * SEQ, D))
        in_maps.append(m)
    res = run_bass_kernel_spmd(nc, in_maps, core_ids=list(range(n_cores)))
    outs = [np.asarray(r['out'], np.float32).reshape(NSEQ, SEQ, D) for r in res.results]
    return np.concatenate(outs, axis=0).reshape(B, SEQ, D)
```
